# Optimizing a Trainium2 kernel written in Bass

```python
import jax, jax.numpy as jnp
from jax import lax
import numpy as np

D_MODEL = 2048
BATCH = 2
SEQ = 4096
DEPTH = 1

CHUNK = 64
Q_BLOCK = 2 * CHUNK
POOL_WIDTH = D_MODEL // 2
POOL_WINDOWS = (2, 4, 8, 16)
POOL_GROUPS = len(POOL_WINDOWS)
POOL_GROUP_WIDTH = POOL_WIDTH // POOL_GROUPS
SB_WIDTH = D_MODEL - POOL_WIDTH
SB_HEAD_DIM = 128
SB_HEADS = SB_WIDTH // SB_HEAD_DIM
MIX_WIDTH = POOL_WIDTH + SB_WIDTH
IN_PROJ_WIDTH = POOL_WIDTH + 3 * SB_WIDTH
D_FF = 4 * D_MODEL
DEEPNORM_ALPHA = (2.0 * DEPTH) ** 0.25
DEEPNORM_BETA = (8.0 * DEPTH) ** -0.25
LN_EPS = 1e-5

kernel_name = "hybrid_pool_stickbreaking_deepnorm_block"


def layer_norm(x, g, b):
    xf = x.astype(jnp.float32)
    mu = jnp.mean(xf, axis=-1, keepdims=True)
    var = jnp.mean(jnp.square(xf - mu), axis=-1, keepdims=True)
    y = (xf - mu) * lax.rsqrt(var + LN_EPS)
    return (y * g.astype(jnp.float32) + b.astype(jnp.float32)).astype(x.dtype)


def multi_scale_pool(u, w_pool, pool_scale):
    b, s, _ = u.shape
    ug = u.reshape(b, s, POOL_GROUPS, POOL_GROUP_WIDTH).astype(jnp.float32)
    csum = jnp.concatenate(
        [jnp.zeros((b, 1, POOL_GROUPS, POOL_GROUP_WIDTH), jnp.float32),
         jnp.cumsum(ug, axis=1)], axis=1)
    t = jnp.arange(s, dtype=jnp.int32)
    windows = jnp.asarray(POOL_WINDOWS, dtype=jnp.int32)
    start = jnp.maximum(t[:, None] + 1 - windows[None, :], 0)
    count = (t[:, None] + 1 - start).astype(jnp.float32)
    group_idx = jnp.arange(POOL_GROUPS, dtype=jnp.int32)[None, :]
    c_start = csum[:, start, group_idx]
    mean = (csum[:, 1:] - c_start) / count[None, :, :, None]
    y = mean - ug
    y = jnp.einsum('bsgc,gcd->bsgd', y, w_pool.astype(jnp.float32))
    y = y * pool_scale.astype(jnp.float32)[None, None]
    return y.reshape(b, s, POOL_WIDTH).astype(u.dtype)


def stick_breaking_attention(q, k, v):
    b, s, h, dh = q.shape
    n_blocks = s // Q_BLOCK
    scale = 1.0 / np.sqrt(dh).astype(np.float32)
    qb = q.reshape(b, n_blocks, Q_BLOCK, h, dh).transpose(1, 0, 2, 3, 4)
    key_pos = jnp.arange(s, dtype=jnp.int32)

    def one_block(args):
        qi, i = args
        z = jnp.einsum('bqhd,bkhd->bhqk', qi, k).astype(jnp.float32) * scale
        q_pos = i * Q_BLOCK + jnp.arange(Q_BLOCK, dtype=jnp.int32)
        mask = (key_pos[None, :] < q_pos[:, None])[None, None]
        log_not = jnp.where(mask, jax.nn.log_sigmoid(-z), 0.0)
        after = lax.cumsum(log_not, axis=3, reverse=True) - log_not
        a = jnp.where(mask, jnp.exp(jax.nn.log_sigmoid(z) + after), 0.0)
        return jnp.einsum('bhqk,bkhd->bqhd', a.astype(v.dtype), v)

    out = lax.map(one_block, (qb, jnp.arange(n_blocks, dtype=jnp.int32)))
    return out.transpose(1, 0, 2, 3, 4).reshape(b, s, h * dh)


def setup_inputs(seed: int = 0) -> dict:
    key = jax.random.key(seed)
    ks = jax.random.split(key, 16)
    f32 = jnp.float32
    x = jax.random.normal(ks[0], (BATCH, SEQ, D_MODEL), f32)
    ln_in_g = 1.0 + 0.02 * jax.random.normal(ks[1], (D_MODEL,), f32)
    ln_in_b = 0.02 * jax.random.normal(ks[2], (D_MODEL,), f32)
    w_in = jax.random.normal(ks[3], (DEPTH, D_MODEL, IN_PROJ_WIDTH), f32) * D_MODEL ** -0.5
    w_pool = jax.random.normal(ks[4], (DEPTH, POOL_GROUPS, POOL_GROUP_WIDTH, POOL_GROUP_WIDTH), f32) * POOL_GROUP_WIDTH ** -0.5
    pool_scale = 1.0 + 0.02 * jax.random.normal(ks[5], (DEPTH, POOL_GROUPS, POOL_GROUP_WIDTH), f32)
    w_out = jax.random.normal(ks[6], (DEPTH, MIX_WIDTH, D_MODEL), f32) * (MIX_WIDTH ** -0.5 * DEEPNORM_BETA)
    ln1_g = 1.0 + 0.02 * jax.random.normal(ks[7], (DEPTH, D_MODEL), f32)
    ln1_b = 0.02 * jax.random.normal(ks[8], (DEPTH, D_MODEL), f32)
    w_ff1 = jax.random.normal(ks[9], (DEPTH, D_MODEL, D_FF), f32) * D_MODEL ** -0.5
    b_ff1 = 0.02 * jax.random.normal(ks[10], (DEPTH, D_FF), f32)
    w_ff2 = jax.random.normal(ks[11], (DEPTH, D_FF, D_MODEL), f32) * (D_FF ** -0.5 * DEEPNORM_BETA)
    b_ff2 = 0.02 * jax.random.normal(ks[12], (DEPTH, D_MODEL), f32)
    ln2_g = 1.0 + 0.02 * jax.random.normal(ks[13], (DEPTH, D_MODEL), f32)
    ln2_b = 0.02 * jax.random.normal(ks[14], (DEPTH, D_MODEL), f32)
    return {"x": x, "ln_in_g": ln_in_g, "ln_in_b": ln_in_b, "w_in": w_in,
            "w_pool": w_pool, "pool_scale": pool_scale, "w_out": w_out,
            "ln1_g": ln1_g, "ln1_b": ln1_b, "w_ff1": w_ff1, "b_ff1": b_ff1,
            "w_ff2": w_ff2, "b_ff2": b_ff2, "ln2_g": ln2_g, "ln2_b": ln2_b}


def reference(x, ln_in_g, ln_in_b, w_in, w_pool, pool_scale, w_out,
              ln1_g, ln1_b, w_ff1, b_ff1, w_ff2, b_ff2, ln2_g, ln2_b):
    b, s, _ = x.shape
    h = layer_norm(x, ln_in_g, ln_in_b)
    for layer in range(DEPTH):
        u = jnp.einsum('bsd,de->bse', h, w_in[layer])
        u_pool = u[..., :POOL_WIDTH]
        q, k, v = jnp.split(u[..., POOL_WIDTH:], 3, axis=-1)
        q = q.reshape(b, s, SB_HEADS, SB_HEAD_DIM)
        k = k.reshape(b, s, SB_HEADS, SB_HEAD_DIM)
        v = v.reshape(b, s, SB_HEADS, SB_HEAD_DIM)
        y_pool = multi_scale_pool(u_pool, w_pool[layer], pool_scale[layer])
        y_sb = stick_breaking_attention(q, k, v)
        mix = jnp.concatenate([y_pool, y_sb], axis=-1)
        mix = jnp.einsum('bse,ed->bsd', mix, w_out[layer])
        h = layer_norm(DEEPNORM_ALPHA * h + mix, ln1_g[layer], ln1_b[layer])
        f = jnp.einsum('bsd,df->bsf', h, w_ff1[layer]) + b_ff1[layer]
        f = jnp.square(jax.nn.relu(f))
        f = jnp.einsum('bsf,fd->bsd', f, w_ff2[layer]) + b_ff2[layer]
        h = layer_norm(DEEPNORM_ALPHA * h + f, ln2_g[layer], ln2_b[layer])
    return h
```

```python
import numpy as np
import ml_dtypes
from contextlib import ExitStack
import concourse.bass as bass
import concourse.mybir as mybir
from concourse.bass_utils import run_bass_kernel_spmd

F32 = mybir.dt.float32
BF16 = mybir.dt.bfloat16
AF = mybir.ActivationFunctionType
ALU = mybir.AluOpType
ENG = ['sync', 'scalar', 'gpsimd', 'vector', 'tensor']

D = 2048
NT = 8
T = 1024
TH = 1152
DFF = 8192
ALPHA = float(2.0 ** 0.25)
EPS = 1e-5
QSCALE = float(1.0 / np.sqrt(128.0))
WINDOWS = (2, 4, 8, 16)
MASKVAL = -30000.0

ARENA_BYTES = 212480
OFF_RES = 0
OFF_CONST = 65536
OFF_W = 73728
WSLOT = 16384
NWSLOT = 2
OFF_MIX = 106496
OFF_X = 139264
Z1 = OFF_X + 36864


class Sem:
    def __init__(self, nc, es, name):
        self.h = es.enter_context(nc.semaphore(name))
        self.v = 0


class Prog:
    def __init__(self, nc, es):
        self.nc, self.es = nc, es
        self.q = {e: [] for e in ENG}
        self.waited = {e: {} for e in ENG}
        self.nsem = 0
        self.esem = {}
        self.new_phase()

    def sem(self, name="s"):
        self.nsem += 1
        return Sem(self.nc, self.es, f"{name}{self.nsem}")

    def new_phase(self):
        for e in ('scalar', 'gpsimd', 'vector', 'tensor'):
            self.esem[e] = self.sem(e[0])

    def dep(self, eng, tok):
        if tok is None:
            return
        sem, tgt = tok
        if self.waited[eng].get(sem, 0) >= tgt:
            return
        self.waited[eng][sem] = tgt
        h = sem.h
        self.q[eng].append(lambda e, h=h, tgt=tgt: e.wait_ge(h, tgt))

    def op(self, eng, method, deps=(), sig=True, **kw):
        for d in deps:
            self.dep(eng, d)
        if sig:
            sem = self.esem[eng]
            sem.v += 1
            tok = (sem, sem.v)
            h = sem.h
            self.q[eng].append(lambda e, m=method, kw=kw, h=h: getattr(e, m)(**kw).then_inc(h, 1))
            return tok
        self.q[eng].append(lambda e, m=method, kw=kw: getattr(e, m)(**kw))
        return None

    def dma(self, eng, out, in_, sem, deps=()):
        for d in deps:
            self.dep(eng, d)
        sem.v += 16
        tok = (sem, sem.v)
        h = sem.h
        self.q[eng].append(lambda e, o=out, i=in_, h=h: e.dma_start(out=o, in_=i).then_inc(h, 16))
        return tok

    def custom(self, eng, fn, deps=()):
        for d in deps:
            self.dep(eng, d)
        self.q[eng].append(fn)

    def emit(self, block):
        for name in ENG:
            ops = self.q[name]
            if not ops:
                continue

            def body(e, ops=ops):
                for f in ops:
                    f(e)
            getattr(block, name)(body)


def build(stop=None):
    nc = bass.Bass("TRN2", target_bir_lowering=False)

    def din(name, shape, dt=F32):
        return nc.dram_tensor(name, list(shape), dt, kind="ExternalInput").ap()

    x_in = din("x", [TH, D])
    w_in = din("w_in", [D, 4096])
    w_pool = din("w_pool", [4, 256, 256])
    pscale = din("pool_scale", [8, 128])
    w_out = din("w_out", [D, D])
    w_ff1 = din("w_ff1", [D, DFF])
    b_ff1 = din("b_ff1", [64, 128])
    w_ff2 = din("w_ff2", [DFF, D])
    vecs = {n: din(n, [1, D]) for n in ("ln_in_g", "ln_in_b", "ln1_g", "ln1_b", "b_ff2", "ln2_g", "ln2_b")}
    ident_d = din("ident", [128, 128])
    identb_d = din("identb", [128, 128], BF16)
    mask_d = din("maskneg", [128, 512], BF16)
    rc_d = din("rc", [128, 512])
    hm_d = din("hm", [128, 128])
    out_d = nc.dram_tensor("out", [T, D], F32, kind="ExternalOutput").ap()
    kvl = [nc.dram_tensor("kvl0", [512, 1024], BF16), nc.dram_tensor("kvl1", [512, 1024], BF16),
           nc.dram_tensor("kvl2", [1024, 512], BF16), nc.dram_tensor("kvl3", [1024, 512], BF16)]
    kva = [nc.dram_tensor("kva0", [2048, 1024], BF16), nc.dram_tensor("kva1", [2048, 1024], BF16),
           nc.dram_tensor("kva2", [4096, 512], BF16), nc.dram_tensor("kva3", [4096, 512], BF16)]
    dbg_d = nc.dram_tensor("dbg", [128, 16 * TH], BF16, kind="ExternalOutput").ap() if stop else None

    es = ExitStack()
    with es:
        arena = es.enter_context(nc.sbuf_tensor("arena", [128, ARENA_BYTES // 2], BF16))
        banks = [es.enter_context(nc.psum_tensor(f"ps{k}", [128, 512], F32))[:, :] for k in range(8)]

        def vw(off, nelem, dt, pat=None, **kw):
            if dt == F32:
                a = arena[:, off // 2: off // 2 + nelem * 2].bitcast(F32)
            else:
                a = arena[:, off // 2: off // 2 + nelem]
            if pat:
                a = a.rearrange(pat, **kw)
            return a

        P = Prog(nc, es)
        block = es.enter_context(nc.Block())

        def finish(deps, dbg_src=None):
            fsem = P.sem("fin")
            tk = None
            for i in range(8):
                tk = P.dma('sync', out_d[i * 128:(i + 1) * 128, :], hres[:, i, :], fsem, deps=deps)
            if dbg_src is not None:
                n = dbg_src.shape[1]
                tk = P.dma('sync', dbg_d[:, 0:n], dbg_src, fsem)
            P.dep('sync', tk)
            for m_ in range(wst['next_take'], wst['next_load']):
                P.dep('sync', wst['loads'][m_][1])
            P.emit(block)
            return nc

        hres = vw(OFF_RES, NT * D, F32, "p (i d) -> p i d", i=NT)
        c0 = OFF_CONST
        identf = vw(c0, 128, F32)
        identb = vw(c0 + 512, 128, BF16)
        maskb = vw(c0 + 768, 512, BF16)
        rc_flat = vw(c0 + 1792, 512, F32)
        rc = rc_flat.rearrange("p (g m) -> p g m", g=4)
        hm_flat = vw(c0 + 3840, 128, F32)
        hm = hm_flat.rearrange("p (i m) -> p i m", i=8)
        b1T = vw(c0 + 4352, 64, F32)
        psT = vw(c0 + 4608, 8, F32)
        epsT = vw(c0 + 4640, 1, F32)
        statf = [vw(c0 + 4704 + k * 128, 24, F32) for k in range(2)]
        stat = [a.rearrange("p (a b) -> p a b", a=4) for a in statf]
        mv = [vw(c0 + 4960 + k * 16, 2, F32) for k in range(2)]
        stdv = [vw(c0 + 4992 + k * 16, 1, F32) for k in range(2)]
        rstd = [vw(c0 + 5024 + k * 16, 1, F32) for k in range(2)]
        nmr = [vw(c0 + 5056 + k * 16, 1, F32) for k in range(2)]
        bst = vw(c0 + 5728, 128, F32)
        pst = vw(c0 + 6240, 128, F32)
        wslot = [vw(OFF_W + k * WSLOT, WSLOT // 2, BF16) for k in range(NWSLOT)]
        mixT = vw(OFF_MIX, 16 * T, BF16, "p (c t) -> p c t", c=16)
        h1T = mixT
        h0T = vw(OFF_X, 16 * TH, BF16, "p (c t) -> p c t", c=16)
        xin = vw(Z1, D, F32)
        PA = vw(Z1 + 8192, D, F32)
        PB = vw(Z1 + 16384, D, F32)
        kst = [vw(Z1 + 24576 + k * 2048, 1024, BF16) for k in range(2)]
        vst = [vw(Z1 + 28672 + k * 1024, 512, BF16) for k in range(2)]
        U = [vw(Z1 + k * 4608, 8 * 144, F32, "p (i m) -> p i m", i=8) for k in range(2)]
        SA = vw(Z1 + 9216, 8 * 144, F32, "p (i m) -> p i m", i=8)
        SB = vw(Z1 + 13824, 8 * 144, F32, "p (i m) -> p i m", i=8)
        ypref = [vw(Z1 + 18432 + k * 2048, 1024, BF16) for k in range(2)]
        ypre = [a.rearrange("p (i m) -> p i m", i=8) for a in ypref]
        wp = vw(208384, 2048, BF16, "p (g k d) -> p g k d", g=4, k=2)
        qT = vw(Z1, 8 * T, BF16, "p (h t) -> p h t", h=8)
        KT = [vw(OFF_X + k * 8192, 4096, BF16, "p (j t) -> p j t", j=4) for k in range(2)]
        VH = [vw(OFF_X + 16384 + k * 8192, 4096, BF16, "p (j i d) -> p j i d", j=4, i=8) for k in range(2)]
        Abuf = [vw(OFF_X + 32768 + k * 1024, 512, BF16) for k in range(3)]
        ATs = [vw(OFF_X + 35840, 512, BF16)] + [vw(Z1 + 16384 + k * 1024, 512, BF16) for k in range(2)]
        nbx = [vw(Z1 + 18432 + k * 2064, 513, F32) for k in range(3)]
        pf = [vw(Z1 + 24624 + k * 2064, 513, F32) for k in range(3)]
        PA3 = vw(OFF_X, D, F32)
        PB3 = vw(OFF_X + 8192, D, F32)
        PC3 = vw(OFF_X + 16384, D, F32)
        gT = [vw(OFF_X + 24576 + k * 16384, 8 * T, BF16, "p (c t) -> p c t", c=8) for k in range(2)]
        rtmp = [vw(OFF_X + 57344 + k * 2048, 512, F32) for k in range(3)]

        bank_free = [None] * 8
        rr = {'n': 0}

        def alloc_bank(subset=None):
            subset = subset or list(range(8))
            k = subset[rr['n'] % len(subset)]
            rr['n'] += 1
            P.dep('tensor', bank_free[k])
            return k

        wsem = [P.sem("w") for _ in range(3)]
        wslot3 = wslot + [vw(OFF_MIX + 16384, WSLOT // 2, BF16)]
        wst = {'next_load': 0, 'next_take': 0, 'loads': {}, 'done': {}}

        def win_src(cg):
            return w_in[:, cg * 512:(cg + 1) * 512].rearrange("(c p) e -> p c e", p=128)

        csem = P.sem("c")
        P.dma('sync', identf, ident_d, csem)
        P.dma('sync', identb, identb_d, csem)
        P.dma('sync', maskb, mask_d, csem)
        P.dma('sync', rc_flat, rc_d, csem)
        P.dma('sync', hm_flat, hm_d, csem)
        P.dma('sync', bst[0:64, :], b_ff1, csem)
        P.dma('sync', pst[0:8, :], pscale, csem)
        P.dma('sync', PA, vecs["ln_in_g"][0].partition_broadcast(128), csem)
        ctok = P.dma('sync', PB, vecs["ln_in_b"][0].partition_broadcast(128), csem)
        xsem = [P.sem("x") for _ in range(9)]
        xtok = [None] * 9
        order = [0, 1, 2, 3, 4, 5, 6, 7, 8]
        for i in order:
            dst = hres[:, i, :] if i < 8 else xin
            xtok[i] = P.dma('sync', dst, x_in[i * 128:(i + 1) * 128, :], xsem[i])

        te = P.op('vector', 'memset', ap=epsT, constant=EPS)
        kb = alloc_bank()
        tb = P.op('tensor', 'transpose', deps=[ctok], out=banks[kb][:, 0:64], in_=bst[0:64, :], identity=identf[0:64, 0:64])
        t1 = P.op('vector', 'tensor_copy', deps=[tb], out=b1T, in_=banks[kb][:, 0:64])
        bank_free[kb] = t1
        kb = alloc_bank()
        tb = P.op('tensor', 'transpose', out=banks[kb][:, 0:8], in_=pst[0:8, :], identity=identf[0:8, 0:8])
        t1 = P.op('vector', 'tensor_copy', deps=[tb], out=psT, in_=banks[kb][:, 0:8])
        bank_free[kb] = t1

        def ln_stats(src, k, deps):
            toks = []
            for c in range(4):
                toks.append(P.op('vector', 'bn_stats', deps=deps, out=stat[k][:, c, :], in_=src[:, c * 512:(c + 1) * 512]))
            ta = P.op('vector', 'bn_aggr', deps=[toks[-1]], out=mv[k], in_=statf[k])
            ts = P.op('scalar', 'activation', deps=[ta, te], out=stdv[k], in_=mv[k][:, 1:2], func=AF.Sqrt, bias=epsT, scale=1.0)
            tr = P.op('vector', 'reciprocal', deps=[ts], out=rstd[k], in_=stdv[k])
            tn = P.op('vector', 'tensor_scalar', deps=[tr], out=nmr[k], in0=mv[k][:, 0:1], scalar1=rstd[k], scalar2=-1.0,
                      op0=ALU.mult, op1=ALU.mult)
            return tn

        def ln_apply(src, k, tn, G, Bt, deps):
            t = P.op('scalar', 'activation', deps=[tn] + list(deps), out=src, in_=src, func=AF.Identity, bias=nmr[k], scale=rstd[k])
            t = P.op('gpsimd', 'tensor_tensor', deps=[t], out=src, in0=src, in1=G, op=ALU.mult)
            t = P.op('vector', 'tensor_tensor', deps=[t], out=src, in0=src, in1=Bt, op=ALU.add)
            return t

        class LNPipe:
            def __init__(self, n, src_fn, dep_fn, G, Bt, par_deps, post_fn):
                self.n, self.src_fn, self.dep_fn, self.G, self.Bt = n, src_fn, dep_fn, G, Bt
                self.par_deps, self.post_fn = list(par_deps), post_fn
                self.tn = [None] * n
                self.tg = [None] * n
                self.tb = [None] * n
                self.s = 0

            def step(self):
                step, n = self.s, self.n
                if step >= n + 3:
                    return
                self.s += 1
                if 0 <= step - 1 < n:
                    i = step - 1
                    k = i % 2
                    src = self.src_fn(i)
                    t = P.op('scalar', 'activation', deps=[self.tn[i]] + self.par_deps, out=src, in_=src, func=AF.Identity,
                             bias=nmr[k], scale=rstd[k])
                    self.tg[i] = t
                    if i % 2 == 1:
                        t2 = P.op('gpsimd', 'tensor_tensor', deps=[t], out=src, in0=src, in1=self.G, op=ALU.mult)
                        self.tb[i] = P.op('gpsimd', 'tensor_tensor', deps=[t2], out=src, in0=src, in1=self.Bt, op=ALU.add)
                if step < n:
                    self.tn[step] = ln_stats(self.src_fn(step), step % 2, self.dep_fn(step))
                if 0 <= step - 2 < n and (step - 2) % 2 == 0:
                    i = step - 2
                    src = self.src_fn(i)
                    t2 = P.op('vector', 'tensor_tensor', deps=[self.tg[i]], out=src, in0=src, in1=self.G, op=ALU.mult)
                    self.tb[i] = P.op('vector', 'tensor_tensor', deps=[t2], out=src, in0=src, in1=self.Bt, op=ALU.add)
                if 0 <= step - 3 < n:
                    self.post_fn(step - 3, self.tb[step - 3])

            def finish(self):
                while self.s < self.n + 3:
                    self.step()
                return self.tb

        def ln_pipeline(n, src_fn, dep_fn, G, Bt, par_deps, post_fn):
            return LNPipe(n, src_fn, dep_fn, G, Bt, par_deps, post_fn).finish()

        def transposes(src, dstT, col0, ncols, deps, evac_dep=None):
            toks = []
            tpe = None
            for b4 in range(4):
                kb = alloc_bank()
                for s in range(4):
                    c = b4 * 4 + s
                    tpe = P.op('tensor', 'transpose', deps=deps, sig=(s == 3), out=banks[kb][:, s * 128:(s + 1) * 128],
                               in_=src[:, c * 128:(c + 1) * 128], identity=identf)
                eng = 'scalar' if b4 % 2 == 0 else 'vector'
                o = dstT[:, b4 * 4:(b4 + 1) * 4, col0:col0 + 128]
                i_ = banks[kb].rearrange("p (s m) -> p s m", s=4)
                dd = [tpe] + ([evac_dep] if evac_dep else [])
                if eng == 'scalar':
                    t = P.op('scalar', 'activation', deps=dd, out=o, in_=i_, func=AF.Copy)
                else:
                    t = P.op('vector', 'tensor_copy', deps=dd, out=o, in_=i_)
                bank_free[kb] = t
                toks.append(t)
            return toks, tpe

        cg_order = [4, 5, 6, 7, 0, 1, 2, 3]
        all_w = [win_src(cg) for cg in cg_order]
        all_w += [w_out[:, cg * 512:(cg + 1) * 512].rearrange("(c p) e -> p c e", p=128) for cg in range(4)]
        for f8 in range(8):
            for a in range(2):
                c0_ = f8 * 1024 + a * 512
                all_w.append(w_ff1[:, c0_:c0_ + 512].rearrange("(c p) e -> p c e", p=128))
            for hlf in range(2):
                all_w.append(w_ff2[f8 * 1024:(f8 + 1) * 1024, hlf * 1024:(hlf + 1) * 1024].rearrange("(c p) e -> p c e", p=128))
        all_shapes = [(16, 512)] * 12 + [(16, 512), (16, 512), (8, 1024), (8, 1024)] * 8
        NL = len(all_w)
        slot_seq = [0, 1, 2, 0, 1, 2, 0, 1] + [m % 2 for m in range(8, NL)]
        prev_user = []
        for m in range(NL):
            pu = -1
            for m2 in range(m - 1, -1, -1):
                if slot_seq[m2] == slot_seq[m]:
                    pu = m2
                    break
            prev_user.append(pu)

        def issue_ready(n_done):
            while wst['next_load'] < NL and prev_user[wst['next_load']] <= n_done:
                m = wst['next_load']
                k = slot_seq[m]
                a, b = all_shapes[m]
                dst = wslot3[k].rearrange("p (a b) -> p a b", a=a)
                tok = P.dma('gpsimd', dst, all_w[m], wsem[k], deps=[wst['done'].get(prev_user[m])])
                wst['loads'][m] = (dst, tok, m)
                wst['next_load'] += 1

        def take_weight():
            m = wst['next_take']
            wst['next_take'] += 1
            return wst['loads'][m]

        def wrelease(m, tok):
            wst['done'][m] = tok
            wst['last'] = m

        def prefetch():
            issue_ready(wst['last'])

        issue_ready(-1)
        wpsem = P.sem("wp")
        wptok = P.dma('gpsimd', wp, w_pool.rearrange("g (k p) d -> p g k d", p=128), wpsem)

        h0T_ready = [None] * 9
        tn_l = [None] * 9
        la_l = [None] * 9
        tile_order = [0, 1, 2, 3, 4, 5, 6, 7, 8]
        def _src0(i):
            return hres[:, i, :] if i < 8 else xin

        def _post0(i, tb_):
            col0 = i * 128 if i < 8 else 1024
            h0T_ready[i], _ = transposes(_src0(i), h0T, col0, 128, [tb_, ctok])

        ln_pipeline(9, _src0, lambda i: [xtok[i], ctok], PA, PB, [], _post0)

        all_h0T = [t for lst in h0T_ready for t in lst]
        if stop == 'ln':
            return finish(all_h0T, vw(OFF_X, 16 * TH, BF16))

        kvsem = [P.sem("kv") for _ in range(4)]
        kst_free = [None, None]
        vst_free = [None, None]
        kv_tokens = []
        pe_last = None

        ccsem = P.sem("cc")
        cctoks = [None] * 4

        def allgather(k, deps):
            for d in deps:
                P.dep('gpsimd', d)
            ccsem.v += 1
            cctoks[k] = (ccsem, ccsem.v)
            P.custom('gpsimd', lambda e, h=ccsem.h, k=k: e.collective_compute(
                "AllGather", ALU.bypass, replica_groups=[[0, 1, 2, 3], [4, 5, 6, 7]],
                ins=[kvl[k].ap().opt()], outs=[kva[k].ap().opt()], dma_qos="P3").then_inc(h, 1))

        for cg in (4, 5):
            W, wtok, wk = take_weight()
            for ec in range(4):
                bk = [alloc_bank(), alloc_bank()]
                for dc in range(16):
                    for th in range(2):
                        deps = [wtok] + (h0T_ready[th * 4] + h0T_ready[th * 4 + 1] + h0T_ready[th * 4 + 2] + h0T_ready[th * 4 + 3]
                                         if dc == 0 else [])
                        pe_last = P.op('tensor', 'matmul', deps=deps, sig=(dc == 15), out=banks[bk[th]][:, :],
                                       lhsT=W[:, dc, ec * 128:(ec + 1) * 128], rhs=h0T[:, dc, th * 512:(th + 1) * 512],
                                       start=(dc == 0), stop=(dc == 15))
                        if dc == 15:
                            bank_free[bk[th]] = None
                            kk = (cg - 4) * 4 + ec
                            s = kk % 2
                            if th == 0:
                                t = P.op('scalar', 'activation', deps=[pe_last, kst_free[s]], out=kst[s][:, 0:512], in_=banks[bk[0]][:, :], func=AF.Copy)
                                t_a = t
                                bank_free[bk[0]] = t
                            else:
                                t = P.op('vector', 'tensor_copy', deps=[pe_last, kst_free[s]], out=kst[s][:, 512:1024], in_=banks[bk[1]][:, :])
                                bank_free[bk[1]] = t
                                row0 = ec * 128
                                tk = P.dma('sync', kvl[cg - 4][row0:row0 + 128, :], kst[s], kvsem[s], deps=[t, t_a])
                                kst_free[s] = tk
                                kv_tokens.append(tk)
            wrelease(wk, pe_last)
            allgather(cg - 4, [kst_free[0], kst_free[1]])
            prefetch()

        for cg in (6, 7):
            W, wtok, wk = take_weight()
            for tt in range(8):
                bk = alloc_bank()
                for dc in range(16):
                    pe_last = P.op('tensor', 'matmul', deps=[wtok] + (all_h0T if dc == 0 else []), sig=(dc == 15), out=banks[bk][:, :],
                                   lhsT=h0T[:, dc, tt * 128:(tt + 1) * 128], rhs=W[:, dc, :], start=(dc == 0), stop=(dc == 15))
                s = tt % 2
                if s == 0:
                    t = P.op('scalar', 'activation', deps=[pe_last, vst_free[s]], out=vst[s], in_=banks[bk][:, :], func=AF.Copy)
                else:
                    t = P.op('vector', 'tensor_copy', deps=[pe_last, vst_free[s]], out=vst[s], in_=banks[bk][:, :])
                bank_free[bk] = t
                tk = P.dma('sync', kvl[cg - 4][tt * 128:(tt + 1) * 128, :], vst[s], kvsem[2 + s], deps=[t])
                vst_free[s] = tk
                kv_tokens.append(tk)
            wrelease(wk, pe_last)
            allgather(cg - 4, [vst_free[0], vst_free[1]])
            prefetch()

        if stop == 'kv0':
            return finish([pe_last] + kv_tokens[-4:] + [kst_free[0], kst_free[1]])
        if stop == 'kv':
            for k in range(4):
                P.dep('sync', cctoks[k])
            return finish([pe_last] + kv_tokens[-4:])
        ypre_free = [None, None]
        U_free = [None, None]
        ypre_tok = [None, None]
        pend = {'g': None}

        def pool_linear():
            if pend['g'] is None:
                return
            g, tk0, tk1 = pend['g']
            pend['g'] = None
            pe2 = None
            for oc in range(2):
                bk2 = [alloc_bank(), alloc_bank()]
                for k2 in range(2):
                    for th in range(2):
                        pe2 = P.op('tensor', 'matmul', deps=[wptok, tk0, tk1], sig=(k2 == 1), out=banks[bk2[th]][:, :],
                                   lhsT=wp[:, g, k2, oc * 128:(oc + 1) * 128],
                                   rhs=ypref[k2][:, th * 512:(th + 1) * 512],
                                   start=(k2 == 0), stop=(k2 == 1))
                        if k2 == 1:
                            tev = P.op('scalar', 'activation', deps=[pe2], out=mixT[:, g * 2 + oc, th * 512:(th + 1) * 512],
                                       in_=banks[bk2[th]][:, :], func=AF.Identity, scale=psT[:, g * 2 + oc:g * 2 + oc + 1])
                            bank_free[bk2[th]] = tev
                            pend['tev'] = tev
            ypre_free[0] = pe2
            ypre_free[1] = pe2
            pend['pe'] = pe2

        for cg in (0, 1):
            W, wtok, wk = take_weight()
            for ec in range(4):
                c = cg * 4 + ec
                g = c // 2
                kc = c % 2
                w = WINDOWS[g]
                bk = [alloc_bank(), alloc_bank(), alloc_bank()]
                for dc in range(16):
                    for th in range(3):
                        n0, n1 = (th * 512, th * 512 + 512) if th < 2 else (1024, 1152)
                        ob = banks[bk[th]][:, :] if th < 2 else banks[bk[2]][:, 0:128]
                        pe_last = P.op('tensor', 'matmul', deps=[wtok] + (all_h0T if dc == 0 else []), sig=(dc == 15 and th == 2), out=ob,
                                       lhsT=W[:, dc, ec * 128:(ec + 1) * 128], rhs=h0T[:, dc, n0:n1],
                                       start=(dc == 0), stop=(dc == 15))
                pool_linear()
                u = U[c % 2]
                t0 = P.op('scalar', 'activation', deps=[pe_last, U_free[c % 2]], out=u[:, 0:4, 0:128],
                          in_=banks[bk[0]].rearrange("p (s m) -> p s m", s=4), func=AF.Copy)
                bank_free[bk[0]] = t0
                t1 = P.op('vector', 'tensor_copy', deps=[pe_last, U_free[c % 2]], out=u[:, 4:8, 0:128],
                          in_=banks[bk[1]].rearrange("p (s m) -> p s m", s=4))
                bank_free[bk[1]] = t1
                t2 = P.op('vector', 'tensor_tensor', out=u[:, :, 128:144],
                          in0=banks[bk[2]][:, 0:128].rearrange("p (i m) -> p i m", i=8), in1=hm, op=ALU.mult)
                bank_free[bk[2]] = t2
                t = P.op('gpsimd', 'tensor_tensor', deps=[t0, t2, U_free[(c + 1) % 2]], out=SA[:, :, 0:143], in0=u[:, :, 0:143], in1=u[:, :, 1:144], op=ALU.add)
                cur, oth = SA, SB
                if w >= 4:
                    t = P.op('gpsimd', 'tensor_tensor', deps=[t], out=oth[:, :, 0:141], in0=cur[:, :, 0:141], in1=cur[:, :, 2:143], op=ALU.add)
                    cur, oth = oth, cur
                if w >= 8:
                    t = P.op('gpsimd', 'tensor_tensor', deps=[t], out=oth[:, :, 0:137], in0=cur[:, :, 0:137], in1=cur[:, :, 4:141], op=ALU.add)
                    cur, oth = oth, cur
                if w >= 16:
                    t = P.op('gpsimd', 'tensor_tensor', deps=[t], out=oth[:, :, 0:129], in0=cur[:, :, 0:129], in1=cur[:, :, 8:137], op=ALU.add)
                    cur, oth = oth, cur
                yp = ypre[kc]
                ta = P.op('gpsimd', 'tensor_scalar', deps=[t], out=oth[:, 0:7, 0:128], in0=cur[:, 0:7, 0:128], scalar1=1.0 / w,
                          scalar2=1.0, op0=ALU.mult, op1=ALU.mult)
                tb_ = P.op('gpsimd', 'tensor_tensor', deps=[t], out=oth[:, 7, 0:128], in0=cur[:, 7, 0:128], in1=rc[:, g, :], op=ALU.mult)
                tc_ = P.op('gpsimd', 'tensor_tensor', deps=[ta, tb_, ypre_free[kc]], out=yp[:, :, :], in0=oth[:, :, 0:128], in1=u[:, :, 0:128],
                           op=ALU.subtract)
                U_free[c % 2] = tc_
                ypre_tok[kc] = tc_
                if kc == 1:
                    pend['g'] = (g, ypre_tok[0], ypre_tok[1])
            wrelease(wk, pe_last)
            prefetch()
        pool_linear()
        pool_pe_last = pend['pe']

        if stop == 'pool':
            return finish([pool_pe_last, pend['tev']], vw(OFF_MIX, 16 * T, BF16))
        q_evac = []
        for cg in (2, 3):
            W, wtok, wk = take_weight()
            for ec in range(4):
                hh = (cg - 2) * 4 + ec
                bk = [alloc_bank(), alloc_bank()]
                for dc in range(16):
                    for th in range(2):
                        pe_last = P.op('tensor', 'matmul', deps=[wtok], sig=(dc == 15), out=banks[bk[th]][:, :],
                                       lhsT=W[:, dc, ec * 128:(ec + 1) * 128], rhs=h0T[:, dc, th * 512:(th + 1) * 512],
                                       start=(dc == 0), stop=(dc == 15))
                        if dc == 15:
                            if th == 0:
                                t = P.op('scalar', 'activation', deps=[pe_last, pool_pe_last], out=qT[:, hh, 0:512], in_=banks[bk[0]][:, :], func=AF.Copy)
                            else:
                                t = P.op('vector', 'tensor_copy', deps=[pe_last, pool_pe_last], out=qT[:, hh, 512:1024], in_=banks[bk[1]][:, :])
                            bank_free[bk[th]] = t
                            q_evac.append(t)
            wrelease(wk, pe_last)
            prefetch()
        inproj_pe_last = pe_last

        if stop == 'q':
            return finish(q_evac, vw(Z1, 8 * T, BF16))
        P.new_phase()
        ksem = [P.sem("k") for _ in range(2)]
        vsem = [P.sem("v") for _ in range(2)]
        kvh_free = [None, None]
        kva_k = [kva[k].ap().rearrange("(j r) c -> r j c", j=4) for k in range(2)]
        kva_v = [kva[2 + k].ap().rearrange("(j i p) c -> p j i c", j=4, i=8, p=128) for k in range(2)]

        def load_head(h):
            s = h % 2
            d = [cctoks[h // 4], cctoks[2 + h // 4], inproj_pe_last, kvh_free[s]] + (q_evac if h < 2 else [])
            hl = h % 4
            tk_ = P.dma('sync', KT[s], kva_k[h // 4][hl * 128:(hl + 1) * 128, :, :], ksem[s], deps=d)
            for j4 in range(4):
                tv_ = P.dma('sync', VH[s][:, j4, :, :], kva_v[h // 4][:, j4, :, hl * 128:(hl + 1) * 128], vsem[s])
            return tk_, tv_

        for k in range(3):
            tm = P.op('vector', 'memset', deps=q_evac, ap=nbx[k][:, 0:1], constant=1.0)
        nb_init = tm

        head_tok = {0: load_head(0), 1: load_head(1)}
        pieces = []
        for h in range(8):
            for iq in range(8):
                for i in range(iq, 8):
                    pieces.append((h, iq, i))
        NP_ = len(pieces)
        ZB = [0, 1, 2]
        ATB = [3, 6]
        YB = [4, 5]
        st = {}
        nb_free = [None] * 3
        pf_free = [None] * 3
        A_free = [None] * 3
        ATs_free = [None] * 3
        atps_free = [None, None]
        ybank_of = {}
        y_cnt = {'n': 0}
        atps = [banks[ATB[0]].bitcast(BF16), banks[ATB[1]].bitcast(BF16)]
        P.dep('tensor', bank_free[ATB[0]])
        P.dep('tensor', bank_free[ATB[1]])
        mix_tok = []
        last_av_of_head = {}

        def S123(n):
            h, iq, i = pieces[n]
            first = (i == iq)
            s3 = n % 3
            zb = ZB[n % 3]
            P.dep('tensor', bank_free[zb])
            ktok, vtok = head_tok[h]
            tq = P.op('tensor', 'matmul', deps=[ktok] + q_evac, sig=(not first), out=banks[zb][:, :],
                      lhsT=qT[:, h, iq * 128:(iq + 1) * 128], rhs=KT[h % 2][:, :, i * 128:(i + 1) * 128],
                      start=True, stop=(not first))
            if first:
                tq = P.op('tensor', 'matmul', out=banks[zb][:, :], lhsT=identb, rhs=maskb, start=False, stop=True)
            tsg = P.op('scalar', 'activation', deps=[tq, nb_free[s3], nb_init], out=nbx[s3][:, 1:513], in_=banks[zb][:, :],
                       func=AF.Sigmoid, scale=-QSCALE)
            bank_free[zb] = tsg
            init = 1.0 if first else pf[(n - 1) % 3][:, 512:513]
            dd = [tsg, pf_free[s3]]
            if not first:
                dd.append(st[n - 1]['scan'])
            tsc = P.op('vector', 'tensor_tensor_scan', deps=dd, out=pf[s3], data0=nbx[s3], data1=nbx[s3], initial=init,
                       op0=ALU.mult, op1=ALU.bypass)
            nb_free[s3] = tsc
            st[n] = {'scan': tsc}
            if n >= 1:
                Sdiff(n - 1)

        def Sdiff(n):
            s3 = n % 3
            tdf = P.op('vector', 'tensor_tensor', deps=[st[n]['scan'], A_free[s3]], out=Abuf[s3], in0=pf[s3][:, 0:512],
                       in1=pf[s3][:, 1:513], op=ALU.subtract)
            st[n]['diff'] = tdf
            pf_free[s3] = tdf

        def S45(n):
            h, iq, i = pieces[n]
            s3 = n % 3
            hf = n % 2
            P.dep('tensor', atps_free[hf])
            for jj in range(4):
                tt_ = P.op('tensor', 'transpose', deps=[st[n]['diff']], sig=(jj == 3),
                           out=atps[hf][:, jj * 128:(jj + 1) * 128],
                           in_=Abuf[s3][:, jj * 128:(jj + 1) * 128], identity=identb)
            A_free[s3] = tt_
            tcp = P.op('scalar', 'activation', deps=[tt_, ATs_free[s3]], out=ATs[s3], in_=atps[hf][:, 0:512], func=AF.Copy)
            atps_free[hf] = tcp
            st[n]['at'] = tcp

        def S6(n):
            h, iq, i = pieces[n]
            s3 = n % 3
            first = (i == iq)
            last = (i == 7)
            if first:
                yb = YB[y_cnt['n'] % 2]
                y_cnt['n'] += 1
                ybank_of[(h, iq)] = yb
                P.dep('tensor', bank_free[yb])
            yb = ybank_of[(h, iq)]
            ktok, vtok = head_tok[h]
            for jj in range(4):
                tav = P.op('tensor', 'matmul', deps=[st[n]['at'], vtok], sig=(jj == 3), out=banks[yb][:, 0:128],
                           lhsT=VH[h % 2][:, jj, i, :], rhs=ATs[s3][:, jj * 128:(jj + 1) * 128],
                           start=(first and jj == 0), stop=(last and jj == 3))
            ATs_free[s3] = tav
            if last:
                tm_ = P.op('scalar', 'activation', deps=[tav], out=mixT[:, 8 + h, iq * 128:(iq + 1) * 128], in_=banks[yb][:, 0:128], func=AF.Copy)
                bank_free[yb] = tm_
                mix_tok.append(tm_)
                if iq == 7:
                    last_av_of_head[h] = tav
                    kvh_free[h % 2] = tav
                    if h + 2 < 8:
                        head_tok[h + 2] = load_head(h + 2)
            del st[n]['at']

        for n in range(NP_ + 4):
            if n < NP_:
                S123(n)
            elif n == NP_:
                Sdiff(NP_ - 1)
            if 0 <= n - 2 < NP_:
                S45(n - 2)
            if 0 <= n - 4 < NP_:
                S6(n - 4)

        bank_free[ATB[0]] = atps_free[0]
        bank_free[ATB[1]] = atps_free[1]
        if stop == 'attn':
            return finish(mix_tok[-8:], vw(OFF_MIX, 16 * T, BF16))
        P.new_phase()
        p3sem = P.sem("p3")
        att_done = mix_tok[-1]
        tg1 = P.dma('sync', PA3, vecs["ln1_g"][0].partition_broadcast(128), p3sem, deps=[att_done, last_av_of_head[7], last_av_of_head[6]])
        tb1 = P.dma('sync', PB3, vecs["ln1_b"][0].partition_broadcast(128), p3sem)
        tb2 = P.dma('sync', PC3, vecs["b_ff2"][0].partition_broadcast(128), p3sem)
        partok = tb2
        res_tok = [[None] * 4 for _ in range(8)]
        h1T_ready = [None] * 8
        acc_tok = [None] * 8
        tp_l = [None] * 8
        tile_pe = [None] * 8

        def _post1(i, tb_):
            h1T_ready[i], tp_l[i] = transposes(hres[:, i, :], h1T, i * 128, 128, [tb_, tile_pe[i]], evac_dep=tile_pe[i])
            acc_tok[i] = P.op('vector', 'scalar_tensor_tensor', deps=[tp_l[i], partok], out=hres[:, i, :], in0=hres[:, i, :],
                              scalar=ALPHA, in1=PC3, op0=ALU.mult, op1=ALU.add)

        ln1 = LNPipe(8, lambda i: hres[:, i, :], lambda i: res_tok[i], PA3, PB3, [partok], _post1)
        for cg in range(4):
            W, wtok, wk = take_weight()
            for tt in range(8):
                bk = alloc_bank()
                for ec in range(16):
                    pe_last = P.op('tensor', 'matmul', deps=[wtok] + (mix_tok if ec == 0 else []), sig=(ec == 15), out=banks[bk][:, :],
                                   lhsT=mixT[:, ec, tt * 128:(tt + 1) * 128], rhs=W[:, ec, :], start=(ec == 0), stop=(ec == 15))
                t = P.op('vector', 'scalar_tensor_tensor', deps=[pe_last], out=hres[:, tt, cg * 512:(cg + 1) * 512],
                         in0=hres[:, tt, cg * 512:(cg + 1) * 512], scalar=ALPHA, in1=banks[bk][:, :], op0=ALU.mult, op1=ALU.add)
                bank_free[bk] = t
                res_tok[tt][cg] = t
                if cg == 3:
                    tile_pe[tt] = pe_last
                    ln1.step()
            wrelease(wk, pe_last)
            prefetch()
        outproj_pe_last = pe_last
        la_l = ln1.finish()
        all_h1T = [t for lst in h1T_ready for t in lst]

        if stop == 'p3':
            return finish(acc_tok + all_h1T, vw(OFF_MIX, 16 * T, BF16))
        P.new_phase()
        p4sem = P.sem("p4")
        gT_free = [None, None]
        rt_free = [None] * 3
        tg2 = P.dma('sync', PA3, vecs["ln2_g"][0].partition_broadcast(128), p4sem, deps=[la_l[7]])
        tb2_ = P.dma('sync', PB3, vecs["ln2_b"][0].partition_broadcast(128), p4sem)
        osem = P.sem("o")
        otoks = []

        def _post2(i, tb_):
            otoks.append(P.dma('sync', out_d[i * 128:(i + 1) * 128, :], hres[:, i, :], osem, deps=[tb_]))

        ln2 = None
        rcnt = {'n': 0}
        last_acc = [[acc_tok[tt]] * 4 for tt in range(8)]
        for f8 in range(8):
            g = gT[f8 % 2]
            g_toks = []
            for a in range(2):
                W, wtok, wk = take_weight()
                for fc in range(4):
                    fchunk = f8 * 8 + a * 4 + fc
                    bk = [alloc_bank(), alloc_bank()]
                    for dc in range(16):
                        for th in range(2):
                            pe_last = P.op('tensor', 'matmul', deps=[wtok] + (all_h1T if dc == 0 else []), sig=(dc == 15), out=banks[bk[th]][:, :],
                                           lhsT=W[:, dc, fc * 128:(fc + 1) * 128], rhs=h1T[:, dc, th * 512:(th + 1) * 512],
                                           start=(dc == 0), stop=(dc == 15))
                            if dc == 15:
                                r = rcnt['n'] % 3
                                rcnt['n'] += 1
                                t = P.op('scalar', 'activation', deps=[pe_last, rt_free[r]], out=rtmp[r], in_=banks[bk[th]][:, :], func=AF.Relu,
                                         bias=b1T[:, fchunk:fchunk + 1], scale=1.0)
                                bank_free[bk[th]] = t
                                t2 = P.op('vector', 'tensor_tensor', deps=[t, gT_free[f8 % 2]], out=g[:, a * 4 + fc, th * 512:(th + 1) * 512],
                                          in0=rtmp[r], in1=rtmp[r], op=ALU.mult)
                                rt_free[r] = t2
                                g_toks.append(t2)
                wrelease(wk, pe_last)
                prefetch()
            for hlf in range(2):
                W, wtok, wk = take_weight()
                for tt in range(8):
                    for cl in range(2):
                        bk = alloc_bank()
                        cg = hlf * 2 + cl
                        for fc in range(8):
                            pe_last = P.op('tensor', 'matmul', deps=[wtok] + (g_toks if fc == 0 else []), sig=(fc == 7), out=banks[bk][:, :],
                                           lhsT=g[:, fc, tt * 128:(tt + 1) * 128], rhs=W[:, fc, cl * 512:(cl + 1) * 512],
                                           start=(fc == 0), stop=(fc == 7))
                        t = P.op('vector', 'tensor_tensor', deps=[pe_last, last_acc[tt][cg]], out=hres[:, tt, cg * 512:(cg + 1) * 512],
                                 in0=hres[:, tt, cg * 512:(cg + 1) * 512], in1=banks[bk][:, :], op=ALU.add)
                        bank_free[bk] = t
                        last_acc[tt][cg] = t
                    if f8 == 7 and hlf == 1:
                        if ln2 is None:
                            ln2 = LNPipe(8, lambda i: hres[:, i, :], lambda i: last_acc[i], PA3, PB3, [tb2_], _post2)
                        ln2.step()
                wrelease(wk, pe_last)
                prefetch()
            gT_free[f8 % 2] = pe_last

        ln2.finish()
        otok = otoks[-1]
        P.dep('sync', otok)
        P.emit(block)
    return nc


_NC_CACHE = {}


def _host_layout(inputs):
    x = np.asarray(inputs["x"], dtype=np.float32)
    B, S, _ = x.shape
    xr = x[:, ::-1, :]
    f32 = np.float32
    common = {
        "w_in": np.ascontiguousarray(inputs["w_in"][0], dtype=f32),
        "w_pool": np.ascontiguousarray(inputs["w_pool"][0], dtype=f32),
        "pool_scale": np.ascontiguousarray(inputs["pool_scale"][0], dtype=f32).reshape(8, 128),
        "w_out": np.ascontiguousarray(inputs["w_out"][0], dtype=f32),
        "w_ff1": np.ascontiguousarray(inputs["w_ff1"][0], dtype=f32),
        "b_ff1": np.ascontiguousarray(inputs["b_ff1"][0], dtype=f32).reshape(64, 128),
        "w_ff2": np.ascontiguousarray(inputs["w_ff2"][0], dtype=f32),
        "ln_in_g": np.ascontiguousarray(inputs["ln_in_g"], dtype=f32).reshape(1, D),
        "ln_in_b": np.ascontiguousarray(inputs["ln_in_b"], dtype=f32).reshape(1, D),
        "ln1_g": np.ascontiguousarray(inputs["ln1_g"][0], dtype=f32).reshape(1, D),
        "ln1_b": np.ascontiguousarray(inputs["ln1_b"][0], dtype=f32).reshape(1, D),
        "b_ff2": np.ascontiguousarray(inputs["b_ff2"][0], dtype=f32).reshape(1, D),
        "ln2_g": np.ascontiguousarray(inputs["ln2_g"][0], dtype=f32).reshape(1, D),
        "ln2_b": np.ascontiguousarray(inputs["ln2_b"][0], dtype=f32).reshape(1, D),
        "ident": np.eye(128, dtype=f32),
        "identb": np.eye(128, dtype=f32).astype(ml_dtypes.bfloat16),
    }
    in_maps = []
    for c in range(8):
        b, j = c // 4, c % 4
        xc = np.zeros((TH, D), dtype=f32)
        hm = np.zeros((128, 128), dtype=f32)
        for i in range(8):
            g = 4 * i + j
            xc[i * 128:(i + 1) * 128] = xr[b, g * 128:(g + 1) * 128]
            if g + 1 < 32:
                xc[1024 + i * 16:1024 + (i + 1) * 16] = xr[b, (g + 1) * 128:(g + 1) * 128 + 16]
                hm[:, i * 16:(i + 1) * 16] = 1.0
        mask = np.zeros((128, 512), dtype=f32)
        for jj in range(4):
            if jj < j:
                mask[:, jj * 128:(jj + 1) * 128] = MASKVAL
            elif jj == j:
                qq = np.arange(128)[:, None]
                kk = np.arange(128)[None, :]
                mask[:, jj * 128:(jj + 1) * 128] = np.where(kk <= qq, MASKVAL, 0.0)
        rcv = np.zeros((128, 4, 128), dtype=f32)
        g7 = 28 + j
        for gi, w in enumerate(WINDOWS):
            r = g7 * 128 + np.arange(128)
            t = (S - 1) - r
            cnt = np.minimum(t + 1, w).astype(f32)
            rcv[:, gi, :] = (1.0 / cnt)[None, :]
        m = dict(common)
        m["x"] = xc
        m["maskneg"] = mask.astype(ml_dtypes.bfloat16)
        m["rc"] = rcv.reshape(128, 512)
        m["hm"] = hm
        in_maps.append(m)
    return in_maps


def kernel(**inputs):
    if "nc" not in _NC_CACHE:
        _NC_CACHE["nc"] = build()
    nc = _NC_CACHE["nc"]
    in_maps = _host_layout(inputs)
    res = run_bass_kernel_spmd(nc, in_maps, core_ids=list(range(8)))
    x = inputs["x"]
    B, S, _ = x.shape
    out_r = np.zeros((B, S, D), dtype=np.float32)
    for c in range(8):
        b, j = c // 4, c % 4
        oc = np.asarray(res.results[c]["out"], dtype=np.float32)
        for i in range(8):
            g = 4 * i + j
            out_r[b, g * 128:(g + 1) * 128] = oc[i * 128:(i + 1) * 128]
    return np.ascontiguousarray(out_r[:, ::-1, :])
```

```python
import numpy as np
import ml_dtypes
from contextlib import ExitStack
import concourse.bass as bass
import concourse.mybir as mybir
from concourse.bass_utils import run_bass_kernel_spmd

F32 = mybir.dt.float32
BF16 = mybir.dt.bfloat16
AF = mybir.ActivationFunctionType
ALU = mybir.AluOpType
ENG = ['sync', 'scalar', 'gpsimd', 'vector', 'tensor']

D = 2048
NT = 8
T = 1024
TH = 1152
DFF = 8192
ALPHA = float(2.0 ** 0.25)
EPS = 1e-5
QSCALE = float(1.0 / np.sqrt(128.0))
WINDOWS = (2, 4, 8, 16)
MASKVAL = -30000.0

ARENA_BYTES = 212480
OFF_RES = 0
OFF_CONST = 65536
OFF_W = 73728
WSLOT = 16384
NWSLOT = 2
OFF_MIX = 106496
OFF_X = 139264
Z1 = OFF_X + 36864


class Sem:
    def __init__(self, nc, es, name):
        self.h = es.enter_context(nc.semaphore(name))
        self.v = 0


class Prog:
    def __init__(self, nc, es):
        self.nc, self.es = nc, es
        self.q = {e: [] for e in ENG}
        self.waited = {e: {} for e in ENG}
        self.nsem = 0
        self.esem = {}
        self.new_phase()

    def sem(self, name="s"):
        self.nsem += 1
        return Sem(self.nc, self.es, f"{name}{self.nsem}")

    def new_phase(self):
        for e in ('scalar', 'gpsimd', 'vector', 'tensor'):
            self.esem[e] = self.sem(e[0])

    def dep(self, eng, tok):
        if tok is None:
            return
        sem, tgt = tok
        if self.waited[eng].get(sem, 0) >= tgt:
            return
        self.waited[eng][sem] = tgt
        h = sem.h
        self.q[eng].append(lambda e, h=h, tgt=tgt: e.wait_ge(h, tgt))

    def op(self, eng, method, deps=(), sig=True, **kw):
        for d in deps:
            self.dep(eng, d)
        if sig:
            sem = self.esem[eng]
            sem.v += 1
            tok = (sem, sem.v)
            h = sem.h
            self.q[eng].append(lambda e, m=method, kw=kw, h=h: getattr(e, m)(**kw).then_inc(h, 1))
            return tok
        self.q[eng].append(lambda e, m=method, kw=kw: getattr(e, m)(**kw))
        return None

    def dma(self, eng, out, in_, sem, deps=()):
        for d in deps:
            self.dep(eng, d)
        sem.v += 16
        tok = (sem, sem.v)
        h = sem.h
        self.q[eng].append(lambda e, o=out, i=in_, h=h: e.dma_start(out=o, in_=i).then_inc(h, 16))
        return tok

    def custom(self, eng, fn, deps=()):
        for d in deps:
            self.dep(eng, d)
        self.q[eng].append(fn)

    def emit(self, block):
        for name in ENG:
            ops = self.q[name]
            if not ops:
                continue

            def body(e, ops=ops):
                for f in ops:
                    f(e)
            getattr(block, name)(body)


def build(stop=None):
    nc = bass.Bass("TRN2", target_bir_lowering=False)

    def din(name, shape, dt=F32):
        return nc.dram_tensor(name, list(shape), dt, kind="ExternalInput").ap()

    x_in = din("x", [TH, D])
    w_in = din("w_in", [D, 4096])
    w_pool = din("w_pool", [4, 256, 256])
    pscale = din("pool_scale", [8, 128])
    w_out = din("w_out", [D, D])
    w_ff1 = din("w_ff1", [D, DFF])
    b_ff1 = din("b_ff1", [64, 128])
    w_ff2 = din("w_ff2", [DFF, D])
    vecs = {n: din(n, [1, D]) for n in ("ln_in_g", "ln_in_b", "ln1_g", "ln1_b", "b_ff2", "ln2_g", "ln2_b")}
    ident_d = din("ident", [128, 128])
    identb_d = din("identb", [128, 128], BF16)
    mask_d = din("maskneg", [128, 512], BF16)
    rc_d = din("rc", [128, 512])
    hm_d = din("hm", [128, 128])
    out_d = nc.dram_tensor("out", [T, D], F32, kind="ExternalOutput").ap()
    kvl = [nc.dram_tensor("kvl0", [512, 1024], BF16), nc.dram_tensor("kvl1", [512, 1024], BF16),
           nc.dram_tensor("kvl2", [1024, 512], BF16), nc.dram_tensor("kvl3", [1024, 512], BF16)]
    kva = [nc.dram_tensor("kva0", [2048, 1024], BF16), nc.dram_tensor("kva1", [2048, 1024], BF16),
           nc.dram_tensor("kva2", [4096, 512], BF16), nc.dram_tensor("kva3", [4096, 512], BF16)]
    dbg_d = nc.dram_tensor("dbg", [128, 16 * TH], BF16, kind="ExternalOutput").ap() if stop else None

    es = ExitStack()
    with es:
        arena = es.enter_context(nc.sbuf_tensor("arena", [128, ARENA_BYTES // 2], BF16))
        banks = [es.enter_context(nc.psum_tensor(f"ps{k}", [128, 512], F32))[:, :] for k in range(8)]

        def vw(off, nelem, dt, pat=None, **kw):
            if dt == F32:
                a = arena[:, off // 2: off // 2 + nelem * 2].bitcast(F32)
            else:
                a = arena[:, off // 2: off // 2 + nelem]
            if pat:
                a = a.rearrange(pat, **kw)
            return a

        P = Prog(nc, es)
        block = es.enter_context(nc.Block())

        def finish(deps, dbg_src=None):
            fsem = P.sem("fin")
            tk = None
            for i in range(8):
                tk = P.dma('sync', out_d[i * 128:(i + 1) * 128, :], hres[:, i, :], fsem, deps=deps)
            if dbg_src is not None:
                n = dbg_src.shape[1]
                tk = P.dma('sync', dbg_d[:, 0:n], dbg_src, fsem)
            P.dep('sync', tk)
            for m_ in range(wst['next_take'], wst['next_load']):
                P.dep('sync', wst['loads'][m_][1])
            P.emit(block)
            return nc

        hres = vw(OFF_RES, NT * D, F32, "p (i d) -> p i d", i=NT)
        c0 = OFF_CONST
        identf = vw(c0, 128, F32)
        identb = vw(c0 + 512, 128, BF16)
        maskb = vw(c0 + 768, 512, BF16)
        rc_flat = vw(c0 + 1792, 512, F32)
        rc = rc_flat.rearrange("p (g m) -> p g m", g=4)
        hm_flat = vw(c0 + 3840, 128, F32)
        hm = hm_flat.rearrange("p (i m) -> p i m", i=8)
        b1T = vw(c0 + 4352, 64, F32)
        psT = vw(c0 + 4608, 8, F32)
        epsT = vw(c0 + 4640, 1, F32)
        statf = [vw(c0 + 4704 + k * 128, 24, F32) for k in range(2)]
        stat = [a.rearrange("p (a b) -> p a b", a=4) for a in statf]
        mv = [vw(c0 + 4960 + k * 16, 2, F32) for k in range(2)]
        stdv = [vw(c0 + 4992 + k * 16, 1, F32) for k in range(2)]
        rstd = [vw(c0 + 5024 + k * 16, 1, F32) for k in range(2)]
        nmr = [vw(c0 + 5056 + k * 16, 1, F32) for k in range(2)]
        bst = vw(c0 + 5728, 128, F32)
        pst = vw(c0 + 6240, 128, F32)
        wslot = [vw(OFF_W + k * WSLOT, WSLOT // 2, BF16) for k in range(NWSLOT)]
        mixT = vw(OFF_MIX, 16 * T, BF16, "p (c t) -> p c t", c=16)
        h1T = mixT
        h0T = vw(OFF_X, 16 * TH, BF16, "p (c t) -> p c t", c=16)
        xin = vw(Z1, D, F32)
        PA = vw(Z1 + 8192, D, F32)
        PB = vw(Z1 + 16384, D, F32)
        kst = [vw(Z1 + 24576 + k * 2048, 1024, BF16) for k in range(2)]
        vst = [vw(Z1 + 28672 + k * 1024, 512, BF16) for k in range(2)]
        U = [vw(Z1 + k * 4608, 8 * 144, F32, "p (i m) -> p i m", i=8) for k in range(2)]
        SA = vw(Z1 + 9216, 8 * 144, F32, "p (i m) -> p i m", i=8)
        SB = vw(Z1 + 13824, 8 * 144, F32, "p (i m) -> p i m", i=8)
        ypref = [vw(Z1 + 18432 + k * 2048, 1024, BF16) for k in range(2)]
        ypre = [a.rearrange("p (i m) -> p i m", i=8) for a in ypref]
        wp = vw(208384, 2048, BF16, "p (g k d) -> p g k d", g=4, k=2)
        qT = vw(Z1, 8 * T, BF16, "p (h t) -> p h t", h=8)
        KT = [vw(OFF_X + k * 8192, 4096, BF16, "p (j t) -> p j t", j=4) for k in range(2)]
        VH = [vw(OFF_X + 16384 + k * 8192, 4096, BF16, "p (j i d) -> p j i d", j=4, i=8) for k in range(2)]
        Abuf = [vw(OFF_X + 32768 + k * 1024, 512, BF16) for k in range(3)]
        ATs = [vw(OFF_X + 35840, 512, BF16)] + [vw(Z1 + 16384 + k * 1024, 512, BF16) for k in range(2)]
        nbx = [vw(Z1 + 18432 + k * 2064, 513, F32) for k in range(3)]
        pf = [vw(Z1 + 24624 + k * 2064, 513, F32) for k in range(3)]
        PA3 = vw(OFF_X, D, F32)
        PB3 = vw(OFF_X + 8192, D, F32)
        PC3 = vw(OFF_X + 16384, D, F32)
        gT = [vw(OFF_X + 24576 + k * 16384, 8 * T, BF16, "p (c t) -> p c t", c=8) for k in range(2)]
        rtmp = [vw(OFF_X + 57344 + k * 2048, 512, F32) for k in range(3)]

        bank_free = [None] * 8
        rr = {'n': 0}

        def alloc_bank(subset=None):
            subset = subset or list(range(8))
            k = subset[rr['n'] % len(subset)]
            rr['n'] += 1
            P.dep('tensor', bank_free[k])
            return k

        wsem = [P.sem("w") for _ in range(3)]
        wslot3 = wslot + [vw(OFF_MIX + 16384, WSLOT // 2, BF16)]
        wst = {'next_load': 0, 'next_take': 0, 'loads': {}, 'done': {}}

        def win_src(cg):
            return w_in[:, cg * 512:(cg + 1) * 512].rearrange("(c p) e -> p c e", p=128)

        csem = P.sem("c")
        P.dma('sync', identf, ident_d, csem)
        P.dma('sync', identb, identb_d, csem)
        P.dma('sync', maskb, mask_d, csem)
        P.dma('sync', rc_flat, rc_d, csem)
        P.dma('sync', hm_flat, hm_d, csem)
        P.dma('sync', bst[0:64, :], b_ff1, csem)
        P.dma('sync', pst[0:8, :], pscale, csem)
        P.dma('sync', PA, vecs["ln_in_g"][0].partition_broadcast(128), csem)
        ctok = P.dma('sync', PB, vecs["ln_in_b"][0].partition_broadcast(128), csem)
        xsem = [P.sem("x") for _ in range(9)]
        xtok = [None] * 9
        order = [0, 1, 2, 3, 4, 5, 6, 7, 8]
        for i in order:
            dst = hres[:, i, :] if i < 8 else xin
            xtok[i] = P.dma('sync', dst, x_in[i * 128:(i + 1) * 128, :], xsem[i])

        te = P.op('vector', 'memset', ap=epsT, constant=EPS)
        kb = alloc_bank()
        tb = P.op('tensor', 'transpose', deps=[ctok], out=banks[kb][:, 0:64], in_=bst[0:64, :], identity=identf[0:64, 0:64])
        t1 = P.op('vector', 'tensor_copy', deps=[tb], out=b1T, in_=banks[kb][:, 0:64])
        bank_free[kb] = t1
        kb = alloc_bank()
        tb = P.op('tensor', 'transpose', out=banks[kb][:, 0:8], in_=pst[0:8, :], identity=identf[0:8, 0:8])
        t1 = P.op('vector', 'tensor_copy', deps=[tb], out=psT, in_=banks[kb][:, 0:8])
        bank_free[kb] = t1

        def ln_stats(src, k, deps):
            toks = []
            for c in range(4):
                toks.append(P.op('vector', 'bn_stats', deps=deps, out=stat[k][:, c, :], in_=src[:, c * 512:(c + 1) * 512]))
            ta = P.op('vector', 'bn_aggr', deps=[toks[-1]], out=mv[k], in_=statf[k])
            ts = P.op('scalar', 'activation', deps=[ta, te], out=stdv[k], in_=mv[k][:, 1:2], func=AF.Sqrt, bias=epsT, scale=1.0)
            tr = P.op('vector', 'reciprocal', deps=[ts], out=rstd[k], in_=stdv[k])
            tn = P.op('vector', 'tensor_scalar', deps=[tr], out=nmr[k], in0=mv[k][:, 0:1], scalar1=rstd[k], scalar2=-1.0,
                      op0=ALU.mult, op1=ALU.mult)
            return tn

        def ln_apply(src, k, tn, G, Bt, deps):
            t = P.op('scalar', 'activation', deps=[tn] + list(deps), out=src, in_=src, func=AF.Identity, bias=nmr[k], scale=rstd[k])
            t = P.op('gpsimd', 'tensor_tensor', deps=[t], out=src, in0=src, in1=G, op=ALU.mult)
            t = P.op('vector', 'tensor_tensor', deps=[t], out=src, in0=src, in1=Bt, op=ALU.add)
            return t

        class LNPipe:
            def __init__(self, n, src_fn, dep_fn, G, Bt, par_deps, post_fn):
                self.n, self.src_fn, self.dep_fn, self.G, self.Bt = n, src_fn, dep_fn, G, Bt
                self.par_deps, self.post_fn = list(par_deps), post_fn
                self.tn = [None] * n
                self.tg = [None] * n
                self.tb = [None] * n
                self.s = 0

            def step(self):
                step, n = self.s, self.n
                if step >= n + 3:
                    return
                self.s += 1
                if 0 <= step - 1 < n:
                    i = step - 1
                    k = i % 2
                    src = self.src_fn(i)
                    t = P.op('scalar', 'activation', deps=[self.tn[i]] + self.par_deps, out=src, in_=src, func=AF.Identity,
                             bias=nmr[k], scale=rstd[k])
                    self.tg[i] = P.op('gpsimd', 'tensor_tensor', deps=[t], out=src, in0=src, in1=self.G, op=ALU.mult)
                if step < n:
                    self.tn[step] = ln_stats(self.src_fn(step), step % 2, self.dep_fn(step))
                if 0 <= step - 2 < n:
                    i = step - 2
                    src = self.src_fn(i)
                    self.tb[i] = P.op('vector', 'tensor_tensor', deps=[self.tg[i]], out=src, in0=src, in1=self.Bt, op=ALU.add)
                if 0 <= step - 3 < n:
                    self.post_fn(step - 3, self.tb[step - 3])

            def finish(self):
                while self.s < self.n + 3:
                    self.step()
                return self.tb

        def ln_pipeline(n, src_fn, dep_fn, G, Bt, par_deps, post_fn):
            return LNPipe(n, src_fn, dep_fn, G, Bt, par_deps, post_fn).finish()

        def transposes(src, dstT, col0, ncols, deps, evac_dep=None):
            toks = []
            tpe = None
            for b4 in range(4):
                kb = alloc_bank()
                for s in range(4):
                    c = b4 * 4 + s
                    tpe = P.op('tensor', 'transpose', deps=deps, sig=(s == 3), out=banks[kb][:, s * 128:(s + 1) * 128],
                               in_=src[:, c * 128:(c + 1) * 128], identity=identf)
                eng = 'scalar' if b4 % 2 == 0 else 'vector'
                o = dstT[:, b4 * 4:(b4 + 1) * 4, col0:col0 + 128]
                i_ = banks[kb].rearrange("p (s m) -> p s m", s=4)
                dd = [tpe] + ([evac_dep] if evac_dep else [])
                if eng == 'scalar':
                    t = P.op('scalar', 'activation', deps=dd, out=o, in_=i_, func=AF.Copy)
                else:
                    t = P.op('vector', 'tensor_copy', deps=dd, out=o, in_=i_)
                bank_free[kb] = t
                toks.append(t)
            return toks, tpe

        cg_order = [4, 5, 6, 7, 0, 1, 2, 3]
        all_w = [win_src(cg) for cg in cg_order]
        all_w += [w_out[:, cg * 512:(cg + 1) * 512].rearrange("(c p) e -> p c e", p=128) for cg in range(4)]
        for f8 in range(8):
            for a in range(2):
                c0_ = f8 * 1024 + a * 512
                all_w.append(w_ff1[:, c0_:c0_ + 512].rearrange("(c p) e -> p c e", p=128))
            for hlf in range(2):
                all_w.append(w_ff2[f8 * 1024:(f8 + 1) * 1024, hlf * 1024:(hlf + 1) * 1024].rearrange("(c p) e -> p c e", p=128))
        all_shapes = [(16, 512)] * 12 + [(16, 512), (16, 512), (8, 1024), (8, 1024)] * 8
        NL = len(all_w)
        slot_seq = [0, 1, 2, 0, 1, 2, 0, 1] + [m % 2 for m in range(8, NL)]
        prev_user = []
        for m in range(NL):
            pu = -1
            for m2 in range(m - 1, -1, -1):
                if slot_seq[m2] == slot_seq[m]:
                    pu = m2
                    break
            prev_user.append(pu)

        def issue_ready(n_done):
            while wst['next_load'] < NL and prev_user[wst['next_load']] <= n_done:
                m = wst['next_load']
                k = slot_seq[m]
                a, b = all_shapes[m]
                dst = wslot3[k].rearrange("p (a b) -> p a b", a=a)
                tok = P.dma('gpsimd', dst, all_w[m], wsem[k], deps=[wst['done'].get(prev_user[m])])
                wst['loads'][m] = (dst, tok, m)
                wst['next_load'] += 1

        def take_weight():
            m = wst['next_take']
            wst['next_take'] += 1
            return wst['loads'][m]

        def wrelease(m, tok):
            wst['done'][m] = tok
            wst['last'] = m

        def prefetch():
            issue_ready(wst['last'])

        issue_ready(-1)
        wpsem = P.sem("wp")
        wptok = P.dma('gpsimd', wp, w_pool.rearrange("g (k p) d -> p g k d", p=128), wpsem)

        h0T_ready = [None] * 9
        tn_l = [None] * 9
        la_l = [None] * 9
        tile_order = [0, 1, 2, 3, 4, 5, 6, 7, 8]
        def _src0(i):
            return hres[:, i, :] if i < 8 else xin

        def _post0(i, tb_):
            col0 = i * 128 if i < 8 else 1024
            h0T_ready[i], _ = transposes(_src0(i), h0T, col0, 128, [tb_, ctok])

        ln_pipeline(9, _src0, lambda i: [xtok[i], ctok], PA, PB, [], _post0)

        all_h0T = [t for lst in h0T_ready for t in lst]
        if stop == 'ln':
            return finish(all_h0T, vw(OFF_X, 16 * TH, BF16))

        kvsem = [P.sem("kv") for _ in range(4)]
        kst_free = [None, None]
        vst_free = [None, None]
        kv_tokens = []
        pe_last = None

        ccsem = P.sem("cc")
        cctoks = [None] * 4

        def allgather(k, deps):
            for d in deps:
                P.dep('gpsimd', d)
            ccsem.v += 1
            cctoks[k] = (ccsem, ccsem.v)
            P.custom('gpsimd', lambda e, h=ccsem.h, k=k: e.collective_compute(
                "AllGather", ALU.bypass, replica_groups=[[0, 1, 2, 3], [4, 5, 6, 7]],
                ins=[kvl[k].ap().opt()], outs=[kva[k].ap().opt()], dma_qos="P2").then_inc(h, 1))

        for cg in (4, 5):
            W, wtok, wk = take_weight()
            for ec in range(4):
                bk = [alloc_bank(), alloc_bank()]
                for dc in range(16):
                    for th in range(2):
                        deps = [wtok] + (h0T_ready[th * 4] + h0T_ready[th * 4 + 1] + h0T_ready[th * 4 + 2] + h0T_ready[th * 4 + 3]
                                         if dc == 0 else [])
                        pe_last = P.op('tensor', 'matmul', deps=deps, sig=(dc == 15), out=banks[bk[th]][:, :],
                                       lhsT=W[:, dc, ec * 128:(ec + 1) * 128], rhs=h0T[:, dc, th * 512:(th + 1) * 512],
                                       start=(dc == 0), stop=(dc == 15))
                        if dc == 15:
                            bank_free[bk[th]] = None
                            kk = (cg - 4) * 4 + ec
                            s = kk % 2
                            if th == 0:
                                t = P.op('scalar', 'activation', deps=[pe_last, kst_free[s]], out=kst[s][:, 0:512], in_=banks[bk[0]][:, :], func=AF.Copy)
                                t_a = t
                                bank_free[bk[0]] = t
                            else:
                                t = P.op('vector', 'tensor_copy', deps=[pe_last, kst_free[s]], out=kst[s][:, 512:1024], in_=banks[bk[1]][:, :])
                                bank_free[bk[1]] = t
                                row0 = ec * 128
                                tk = P.dma('sync', kvl[cg - 4][row0:row0 + 128, :], kst[s], kvsem[s], deps=[t, t_a])
                                kst_free[s] = tk
                                kv_tokens.append(tk)
            wrelease(wk, pe_last)
            allgather(cg - 4, [kst_free[0], kst_free[1]])
            prefetch()

        for cg in (6, 7):
            W, wtok, wk = take_weight()
            for tt in range(8):
                bk = alloc_bank()
                for dc in range(16):
                    pe_last = P.op('tensor', 'matmul', deps=[wtok] + (all_h0T if dc == 0 else []), sig=(dc == 15), out=banks[bk][:, :],
                                   lhsT=h0T[:, dc, tt * 128:(tt + 1) * 128], rhs=W[:, dc, :], start=(dc == 0), stop=(dc == 15))
                s = tt % 2
                if s == 0:
                    t = P.op('scalar', 'activation', deps=[pe_last, vst_free[s]], out=vst[s], in_=banks[bk][:, :], func=AF.Copy)
                else:
                    t = P.op('vector', 'tensor_copy', deps=[pe_last, vst_free[s]], out=vst[s], in_=banks[bk][:, :])
                bank_free[bk] = t
                tk = P.dma('sync', kvl[cg - 4][tt * 128:(tt + 1) * 128, :], vst[s], kvsem[2 + s], deps=[t])
                vst_free[s] = tk
                kv_tokens.append(tk)
            wrelease(wk, pe_last)
            allgather(cg - 4, [vst_free[0], vst_free[1]])
            prefetch()

        if stop == 'kv0':
            return finish([pe_last] + kv_tokens[-4:] + [kst_free[0], kst_free[1]])
        if stop == 'kv':
            for k in range(4):
                P.dep('sync', cctoks[k])
            return finish([pe_last] + kv_tokens[-4:])
        ypre_free = [None, None]
        U_free = [None, None]
        ypre_tok = [None, None]
        pend = {'g': None}

        def pool_linear():
            if pend['g'] is None:
                return
            g, tk0, tk1 = pend['g']
            pend['g'] = None
            pe2 = None
            for oc in range(2):
                bk2 = [alloc_bank(), alloc_bank()]
                for k2 in range(2):
                    for th in range(2):
                        pe2 = P.op('tensor', 'matmul', deps=[wptok, tk0, tk1], sig=(k2 == 1), out=banks[bk2[th]][:, :],
                                   lhsT=wp[:, g, k2, oc * 128:(oc + 1) * 128],
                                   rhs=ypref[k2][:, th * 512:(th + 1) * 512],
                                   start=(k2 == 0), stop=(k2 == 1))
                        if k2 == 1:
                            tev = P.op('scalar', 'activation', deps=[pe2], out=mixT[:, g * 2 + oc, th * 512:(th + 1) * 512],
                                       in_=banks[bk2[th]][:, :], func=AF.Identity, scale=psT[:, g * 2 + oc:g * 2 + oc + 1])
                            bank_free[bk2[th]] = tev
                            pend['tev'] = tev
            ypre_free[0] = pe2
            ypre_free[1] = pe2
            pend['pe'] = pe2

        for cg in (0, 1):
            W, wtok, wk = take_weight()
            for ec in range(4):
                c = cg * 4 + ec
                g = c // 2
                kc = c % 2
                w = WINDOWS[g]
                bk = [alloc_bank(), alloc_bank(), alloc_bank()]
                for dc in range(16):
                    for th in range(3):
                        n0, n1 = (th * 512, th * 512 + 512) if th < 2 else (1024, 1152)
                        ob = banks[bk[th]][:, :] if th < 2 else banks[bk[2]][:, 0:128]
                        pe_last = P.op('tensor', 'matmul', deps=[wtok] + (all_h0T if dc == 0 else []), sig=(dc == 15 and th == 2), out=ob,
                                       lhsT=W[:, dc, ec * 128:(ec + 1) * 128], rhs=h0T[:, dc, n0:n1],
                                       start=(dc == 0), stop=(dc == 15))
                pool_linear()
                u = U[c % 2]
                t0 = P.op('scalar', 'activation', deps=[pe_last, U_free[c % 2]], out=u[:, 0:4, 0:128],
                          in_=banks[bk[0]].rearrange("p (s m) -> p s m", s=4), func=AF.Copy)
                bank_free[bk[0]] = t0
                t1 = P.op('vector', 'tensor_copy', deps=[pe_last, U_free[c % 2]], out=u[:, 4:8, 0:128],
                          in_=banks[bk[1]].rearrange("p (s m) -> p s m", s=4))
                bank_free[bk[1]] = t1
                t2 = P.op('vector', 'tensor_tensor', out=u[:, :, 128:144],
                          in0=banks[bk[2]][:, 0:128].rearrange("p (i m) -> p i m", i=8), in1=hm, op=ALU.mult)
                bank_free[bk[2]] = t2
                t = P.op('gpsimd', 'tensor_tensor', deps=[t0, t2, U_free[(c + 1) % 2]], out=SA[:, :, 0:143], in0=u[:, :, 0:143], in1=u[:, :, 1:144], op=ALU.add)
                cur, oth = SA, SB
                if w >= 4:
                    t = P.op('gpsimd', 'tensor_tensor', deps=[t], out=oth[:, :, 0:141], in0=cur[:, :, 0:141], in1=cur[:, :, 2:143], op=ALU.add)
                    cur, oth = oth, cur
                if w >= 8:
                    t = P.op('gpsimd', 'tensor_tensor', deps=[t], out=oth[:, :, 0:137], in0=cur[:, :, 0:137], in1=cur[:, :, 4:141], op=ALU.add)
                    cur, oth = oth, cur
                if w >= 16:
                    t = P.op('gpsimd', 'tensor_tensor', deps=[t], out=oth[:, :, 0:129], in0=cur[:, :, 0:129], in1=cur[:, :, 8:137], op=ALU.add)
                    cur, oth = oth, cur
                yp = ypre[kc]
                ta = P.op('gpsimd', 'tensor_scalar', deps=[t], out=oth[:, 0:7, 0:128], in0=cur[:, 0:7, 0:128], scalar1=1.0 / w,
                          scalar2=1.0, op0=ALU.mult, op1=ALU.mult)
                tb_ = P.op('gpsimd', 'tensor_tensor', deps=[t], out=oth[:, 7, 0:128], in0=cur[:, 7, 0:128], in1=rc[:, g, :], op=ALU.mult)
                tc_ = P.op('gpsimd', 'tensor_tensor', deps=[ta, tb_, ypre_free[kc]], out=yp[:, :, :], in0=oth[:, :, 0:128], in1=u[:, :, 0:128],
                           op=ALU.subtract)
                U_free[c % 2] = tc_
                ypre_tok[kc] = tc_
                if kc == 1:
                    pend['g'] = (g, ypre_tok[0], ypre_tok[1])
            wrelease(wk, pe_last)
            prefetch()
        pool_linear()
        pool_pe_last = pend['pe']

        if stop == 'pool':
            return finish([pool_pe_last, pend['tev']], vw(OFF_MIX, 16 * T, BF16))
        q_evac = []
        for cg in (2, 3):
            W, wtok, wk = take_weight()
            for ec in range(4):
                hh = (cg - 2) * 4 + ec
                bk = [alloc_bank(), alloc_bank()]
                for dc in range(16):
                    for th in range(2):
                        pe_last = P.op('tensor', 'matmul', deps=[wtok], sig=(dc == 15), out=banks[bk[th]][:, :],
                                       lhsT=W[:, dc, ec * 128:(ec + 1) * 128], rhs=h0T[:, dc, th * 512:(th + 1) * 512],
                                       start=(dc == 0), stop=(dc == 15))
                        if dc == 15:
                            if th == 0:
                                t = P.op('scalar', 'activation', deps=[pe_last, pool_pe_last], out=qT[:, hh, 0:512], in_=banks[bk[0]][:, :], func=AF.Copy)
                            else:
                                t = P.op('vector', 'tensor_copy', deps=[pe_last, pool_pe_last], out=qT[:, hh, 512:1024], in_=banks[bk[1]][:, :])
                            bank_free[bk[th]] = t
                            q_evac.append(t)
            wrelease(wk, pe_last)
            prefetch()
        inproj_pe_last = pe_last

        if stop == 'q':
            return finish(q_evac, vw(Z1, 8 * T, BF16))
        P.new_phase()
        ksem = [P.sem("k") for _ in range(2)]
        vsem = [P.sem("v") for _ in range(2)]
        kvh_free = [None, None]
        kva_k = [kva[k].ap().rearrange("(j r) c -> r j c", j=4) for k in range(2)]
        kva_v = [kva[2 + k].ap().rearrange("(j i p) c -> p j i c", j=4, i=8, p=128) for k in range(2)]

        def load_head(h):
            s = h % 2
            d = [cctoks[h // 4], cctoks[2 + h // 4], inproj_pe_last, kvh_free[s]] + (q_evac if h < 2 else [])
            hl = h % 4
            tk_ = P.dma('sync', KT[s], kva_k[h // 4][hl * 128:(hl + 1) * 128, :, :], ksem[s], deps=d)
            for j4 in range(4):
                tv_ = P.dma('sync', VH[s][:, j4, :, :], kva_v[h // 4][:, j4, :, hl * 128:(hl + 1) * 128], vsem[s])
            return tk_, tv_

        for k in range(3):
            tm = P.op('vector', 'memset', deps=q_evac, ap=nbx[k][:, 0:1], constant=1.0)
        nb_init = tm

        head_tok = {0: load_head(0), 1: load_head(1)}
        pieces = []
        for h in range(8):
            for iq in range(8):
                for i in range(iq, 8):
                    pieces.append((h, iq, i))
        NP_ = len(pieces)
        ZB = [0, 1, 2]
        ATB = [3, 6]
        YB = [4, 5]
        st = {}
        nb_free = [None] * 3
        pf_free = [None] * 3
        A_free = [None] * 3
        ATs_free = [None] * 3
        atps_free = [None, None]
        ybank_of = {}
        y_cnt = {'n': 0}
        atps = [banks[ATB[0]].bitcast(BF16), banks[ATB[1]].bitcast(BF16)]
        P.dep('tensor', bank_free[ATB[0]])
        P.dep('tensor', bank_free[ATB[1]])
        mix_tok = []
        last_av_of_head = {}

        def S123(n):
            h, iq, i = pieces[n]
            first = (i == iq)
            s3 = n % 3
            zb = ZB[n % 3]
            P.dep('tensor', bank_free[zb])
            ktok, vtok = head_tok[h]
            tq = P.op('tensor', 'matmul', deps=[ktok] + q_evac, sig=(not first), out=banks[zb][:, :],
                      lhsT=qT[:, h, iq * 128:(iq + 1) * 128], rhs=KT[h % 2][:, :, i * 128:(i + 1) * 128],
                      start=True, stop=(not first))
            if first:
                tq = P.op('tensor', 'matmul', out=banks[zb][:, :], lhsT=identb, rhs=maskb, start=False, stop=True)
            tsg = P.op('scalar', 'activation', deps=[tq, nb_free[s3], nb_init], out=nbx[s3][:, 1:513], in_=banks[zb][:, :],
                       func=AF.Sigmoid, scale=-QSCALE)
            bank_free[zb] = tsg
            init = 1.0 if first else pf[(n - 1) % 3][:, 512:513]
            dd = [tsg, pf_free[s3]]
            if not first:
                dd.append(st[n - 1]['scan'])
            tsc = P.op('vector', 'tensor_tensor_scan', deps=dd, out=pf[s3], data0=nbx[s3], data1=nbx[s3], initial=init,
                       op0=ALU.mult, op1=ALU.bypass)
            nb_free[s3] = tsc
            st[n] = {'scan': tsc}
            if n >= 1:
                Sdiff(n - 1)

        def Sdiff(n):
            s3 = n % 3
            tdf = P.op('vector', 'tensor_tensor', deps=[st[n]['scan'], A_free[s3]], out=Abuf[s3], in0=pf[s3][:, 0:512],
                       in1=pf[s3][:, 1:513], op=ALU.subtract)
            st[n]['diff'] = tdf
            pf_free[s3] = tdf

        def S45(n):
            h, iq, i = pieces[n]
            s3 = n % 3
            hf = n % 2
            P.dep('tensor', atps_free[hf])
            for jj in range(4):
                tt_ = P.op('tensor', 'transpose', deps=[st[n]['diff']], sig=(jj == 3),
                           out=atps[hf][:, jj * 128:(jj + 1) * 128],
                           in_=Abuf[s3][:, jj * 128:(jj + 1) * 128], identity=identb)
            A_free[s3] = tt_
            tcp = P.op('scalar', 'activation', deps=[tt_, ATs_free[s3]], out=ATs[s3], in_=atps[hf][:, 0:512], func=AF.Copy)
            atps_free[hf] = tcp
            st[n]['at'] = tcp

        def S6(n):
            h, iq, i = pieces[n]
            s3 = n % 3
            first = (i == iq)
            last = (i == 7)
            if first:
                yb = YB[y_cnt['n'] % 2]
                y_cnt['n'] += 1
                ybank_of[(h, iq)] = yb
                P.dep('tensor', bank_free[yb])
            yb = ybank_of[(h, iq)]
            ktok, vtok = head_tok[h]
            for jj in range(4):
                tav = P.op('tensor', 'matmul', deps=[st[n]['at'], vtok], sig=(jj == 3), out=banks[yb][:, 0:128],
                           lhsT=VH[h % 2][:, jj, i, :], rhs=ATs[s3][:, jj * 128:(jj + 1) * 128],
                           start=(first and jj == 0), stop=(last and jj == 3))
            ATs_free[s3] = tav
            if last:
                tm_ = P.op('scalar', 'activation', deps=[tav], out=mixT[:, 8 + h, iq * 128:(iq + 1) * 128], in_=banks[yb][:, 0:128], func=AF.Copy)
                bank_free[yb] = tm_
                mix_tok.append(tm_)
                if iq == 7:
                    last_av_of_head[h] = tav
                    kvh_free[h % 2] = tav
                    if h + 2 < 8:
                        head_tok[h + 2] = load_head(h + 2)
            del st[n]['at']

        for n in range(NP_ + 4):
            if n < NP_:
                S123(n)
            elif n == NP_:
                Sdiff(NP_ - 1)
            if 0 <= n - 2 < NP_:
                S45(n - 2)
            if 0 <= n - 4 < NP_:
                S6(n - 4)

        bank_free[ATB[0]] = atps_free[0]
        bank_free[ATB[1]] = atps_free[1]
        if stop == 'attn':
            return finish(mix_tok[-8:], vw(OFF_MIX, 16 * T, BF16))
        P.new_phase()
        p3sem = P.sem("p3")
        att_done = mix_tok[-1]
        tg1 = P.dma('sync', PA3, vecs["ln1_g"][0].partition_broadcast(128), p3sem, deps=[att_done, last_av_of_head[7], last_av_of_head[6]])
        tb1 = P.dma('sync', PB3, vecs["ln1_b"][0].partition_broadcast(128), p3sem)
        tb2 = P.dma('sync', PC3, vecs["b_ff2"][0].partition_broadcast(128), p3sem)
        partok = tb2
        res_tok = [[None] * 4 for _ in range(8)]
        h1T_ready = [None] * 8
        acc_tok = [None] * 8
        tp_l = [None] * 8
        tile_pe = [None] * 8

        def _post1(i, tb_):
            h1T_ready[i], tp_l[i] = transposes(hres[:, i, :], h1T, i * 128, 128, [tb_, tile_pe[i]], evac_dep=tile_pe[i])
            acc_tok[i] = P.op('vector', 'scalar_tensor_tensor', deps=[tp_l[i], partok], out=hres[:, i, :], in0=hres[:, i, :],
                              scalar=ALPHA, in1=PC3, op0=ALU.mult, op1=ALU.add)

        ln1 = LNPipe(8, lambda i: hres[:, i, :], lambda i: res_tok[i], PA3, PB3, [partok], _post1)
        for cg in range(4):
            W, wtok, wk = take_weight()
            for tt in range(8):
                bk = alloc_bank()
                for ec in range(16):
                    pe_last = P.op('tensor', 'matmul', deps=[wtok] + (mix_tok if ec == 0 else []), sig=(ec == 15), out=banks[bk][:, :],
                                   lhsT=mixT[:, ec, tt * 128:(tt + 1) * 128], rhs=W[:, ec, :], start=(ec == 0), stop=(ec == 15))
                t = P.op('vector', 'scalar_tensor_tensor', deps=[pe_last], out=hres[:, tt, cg * 512:(cg + 1) * 512],
                         in0=hres[:, tt, cg * 512:(cg + 1) * 512], scalar=ALPHA, in1=banks[bk][:, :], op0=ALU.mult, op1=ALU.add)
                bank_free[bk] = t
                res_tok[tt][cg] = t
                if cg == 3:
                    tile_pe[tt] = pe_last
                    ln1.step()
            wrelease(wk, pe_last)
            prefetch()
        outproj_pe_last = pe_last
        la_l = ln1.finish()
        all_h1T = [t for lst in h1T_ready for t in lst]

        if stop == 'p3':
            return finish(acc_tok + all_h1T, vw(OFF_MIX, 16 * T, BF16))
        P.new_phase()
        p4sem = P.sem("p4")
        gT_free = [None, None]
        rt_free = [None] * 3
        tg2 = P.dma('sync', PA3, vecs["ln2_g"][0].partition_broadcast(128), p4sem, deps=[la_l[7]])
        tb2_ = P.dma('sync', PB3, vecs["ln2_b"][0].partition_broadcast(128), p4sem)
        osem = P.sem("o")
        otoks = []

        def _post2(i, tb_):
            otoks.append(P.dma('sync', out_d[i * 128:(i + 1) * 128, :], hres[:, i, :], osem, deps=[tb_]))

        ln2 = None
        rcnt = {'n': 0}
        last_acc = [[acc_tok[tt]] * 4 for tt in range(8)]
        for f8 in range(8):
            g = gT[f8 % 2]
            g_toks = []
            for a in range(2):
                W, wtok, wk = take_weight()
                for fc in range(4):
                    fchunk = f8 * 8 + a * 4 + fc
                    bk = [alloc_bank(), alloc_bank()]
                    for dc in range(16):
                        for th in range(2):
                            pe_last = P.op('tensor', 'matmul', deps=[wtok] + (all_h1T if dc == 0 else []), sig=(dc == 15), out=banks[bk[th]][:, :],
                                           lhsT=W[:, dc, fc * 128:(fc + 1) * 128], rhs=h1T[:, dc, th * 512:(th + 1) * 512],
                                           start=(dc == 0), stop=(dc == 15))
                            if dc == 15:
                                r = rcnt['n'] % 3
                                rcnt['n'] += 1
                                t = P.op('scalar', 'activation', deps=[pe_last, rt_free[r]], out=rtmp[r], in_=banks[bk[th]][:, :], func=AF.Relu,
                                         bias=b1T[:, fchunk:fchunk + 1], scale=1.0)
                                bank_free[bk[th]] = t
                                t2 = P.op('vector', 'tensor_tensor', deps=[t, gT_free[f8 % 2]], out=g[:, a * 4 + fc, th * 512:(th + 1) * 512],
                                          in0=rtmp[r], in1=rtmp[r], op=ALU.mult)
                                rt_free[r] = t2
                                g_toks.append(t2)
                wrelease(wk, pe_last)
                prefetch()
            for hlf in range(2):
                W, wtok, wk = take_weight()
                for tt in range(8):
                    for cl in range(2):
                        bk = alloc_bank()
                        cg = hlf * 2 + cl
                        for fc in range(8):
                            pe_last = P.op('tensor', 'matmul', deps=[wtok] + (g_toks if fc == 0 else []), sig=(fc == 7), out=banks[bk][:, :],
                                           lhsT=g[:, fc, tt * 128:(tt + 1) * 128], rhs=W[:, fc, cl * 512:(cl + 1) * 512],
                                           start=(fc == 0), stop=(fc == 7))
                        t = P.op('vector', 'tensor_tensor', deps=[pe_last, last_acc[tt][cg]], out=hres[:, tt, cg * 512:(cg + 1) * 512],
                                 in0=hres[:, tt, cg * 512:(cg + 1) * 512], in1=banks[bk][:, :], op=ALU.add)
                        bank_free[bk] = t
                        last_acc[tt][cg] = t
                    if f8 == 7 and hlf == 1:
                        if ln2 is None:
                            ln2 = LNPipe(8, lambda i: hres[:, i, :], lambda i: last_acc[i], PA3, PB3, [tb2_], _post2)
                        ln2.step()
                wrelease(wk, pe_last)
                prefetch()
            gT_free[f8 % 2] = pe_last

        ln2.finish()
        otok = otoks[-1]
        P.dep('sync', otok)
        P.emit(block)
    return nc


_NC_CACHE = {}


def _host_layout(inputs):
    x = np.asarray(inputs["x"], dtype=np.float32)
    B, S, _ = x.shape
    xr = x[:, ::-1, :]
    f32 = np.float32
    common = {
        "w_in": np.ascontiguousarray(inputs["w_in"][0], dtype=f32),
        "w_pool": np.ascontiguousarray(inputs["w_pool"][0], dtype=f32),
        "pool_scale": np.ascontiguousarray(inputs["pool_scale"][0], dtype=f32).reshape(8, 128),
        "w_out": np.ascontiguousarray(inputs["w_out"][0], dtype=f32),
        "w_ff1": np.ascontiguousarray(inputs["w_ff1"][0], dtype=f32),
        "b_ff1": np.ascontiguousarray(inputs["b_ff1"][0], dtype=f32).reshape(64, 128),
        "w_ff2": np.ascontiguousarray(inputs["w_ff2"][0], dtype=f32),
        "ln_in_g": np.ascontiguousarray(inputs["ln_in_g"], dtype=f32).reshape(1, D),
        "ln_in_b": np.ascontiguousarray(inputs["ln_in_b"], dtype=f32).reshape(1, D),
        "ln1_g": np.ascontiguousarray(inputs["ln1_g"][0], dtype=f32).reshape(1, D),
        "ln1_b": np.ascontiguousarray(inputs["ln1_b"][0], dtype=f32).reshape(1, D),
        "b_ff2": np.ascontiguousarray(inputs["b_ff2"][0], dtype=f32).reshape(1, D),
        "ln2_g": np.ascontiguousarray(inputs["ln2_g"][0], dtype=f32).reshape(1, D),
        "ln2_b": np.ascontiguousarray(inputs["ln2_b"][0], dtype=f32).reshape(1, D),
        "ident": np.eye(128, dtype=f32),
        "identb": np.eye(128, dtype=f32).astype(ml_dtypes.bfloat16),
    }
    in_maps = []
    for c in range(8):
        b, j = c // 4, c % 4
        xc = np.zeros((TH, D), dtype=f32)
        hm = np.zeros((128, 128), dtype=f32)
        for i in range(8):
            g = 4 * i + j
            xc[i * 128:(i + 1) * 128] = xr[b, g * 128:(g + 1) * 128]
            if g + 1 < 32:
                xc[1024 + i * 16:1024 + (i + 1) * 16] = xr[b, (g + 1) * 128:(g + 1) * 128 + 16]
                hm[:, i * 16:(i + 1) * 16] = 1.0
        mask = np.zeros((128, 512), dtype=f32)
        for jj in range(4):
            if jj < j:
                mask[:, jj * 128:(jj + 1) * 128] = MASKVAL
            elif jj == j:
                qq = np.arange(128)[:, None]
                kk = np.arange(128)[None, :]
                mask[:, jj * 128:(jj + 1) * 128] = np.where(kk <= qq, MASKVAL, 0.0)
        rcv = np.zeros((128, 4, 128), dtype=f32)
        g7 = 28 + j
        for gi, w in enumerate(WINDOWS):
            r = g7 * 128 + np.arange(128)
            t = (S - 1) - r
            cnt = np.minimum(t + 1, w).astype(f32)
            rcv[:, gi, :] = (1.0 / cnt)[None, :]
        m = dict(common)
        m["x"] = xc
        m["maskneg"] = mask.astype(ml_dtypes.bfloat16)
        m["rc"] = rcv.reshape(128, 512)
        m["hm"] = hm
        in_maps.append(m)
    return in_maps


def kernel(**inputs):
    if "nc" not in _NC_CACHE:
        _NC_CACHE["nc"] = build()
    nc = _NC_CACHE["nc"]
    in_maps = _host_layout(inputs)
    res = run_bass_kernel_spmd(nc, in_maps, core_ids=list(range(8)))
    x = inputs["x"]
    B, S, _ = x.shape
    out_r = np.zeros((B, S, D), dtype=np.float32)
    for c in range(8):
        b, j = c // 4, c % 4
        oc = np.asarray(res.results[c]["out"], dtype=np.float32)
        for i in range(8):
            g = 4 * i + j
            out_r[b, g * 128:(g + 1) * 128] = oc[i * 128:(i + 1) * 128]
    return np.ascontiguousarray(out_r[:, ::-1, :])
```

```python
import numpy as np
import ml_dtypes
from contextlib import ExitStack
import concourse.bass as bass
import concourse.mybir as mybir
from concourse.bass_utils import run_bass_kernel_spmd

F32 = mybir.dt.float32
BF16 = mybir.dt.bfloat16
AF = mybir.ActivationFunctionType
ALU = mybir.AluOpType
ENG = ['sync', 'scalar', 'gpsimd', 'vector', 'tensor']

D = 2048
NT = 8
T = 1024
TH = 1152
DFF = 8192
ALPHA = float(2.0 ** 0.25)
EPS = 1e-5
QSCALE = float(1.0 / np.sqrt(128.0))
WINDOWS = (2, 4, 8, 16)
MASKVAL = -30000.0

ARENA_BYTES = 212480
OFF_RES = 0
OFF_CONST = 65536
OFF_W = 73728
WSLOT = 16384
NWSLOT = 2
OFF_MIX = 106496
OFF_X = 139264
Z1 = OFF_X + 36864


class Sem:
    def __init__(self, nc, es, name):
        self.h = es.enter_context(nc.semaphore(name))
        self.v = 0


class Prog:
    def __init__(self, nc, es):
        self.nc, self.es = nc, es
        self.q = {e: [] for e in ENG}
        self.waited = {e: {} for e in ENG}
        self.nsem = 0
        self.esem = {}
        self.new_phase()

    def sem(self, name="s"):
        self.nsem += 1
        return Sem(self.nc, self.es, f"{name}{self.nsem}")

    def new_phase(self):
        for e in ('scalar', 'gpsimd', 'vector', 'tensor'):
            self.esem[e] = self.sem(e[0])

    def dep(self, eng, tok):
        if tok is None:
            return
        sem, tgt = tok
        if self.waited[eng].get(sem, 0) >= tgt:
            return
        self.waited[eng][sem] = tgt
        h = sem.h
        self.q[eng].append(lambda e, h=h, tgt=tgt: e.wait_ge(h, tgt))

    def op(self, eng, method, deps=(), sig=True, **kw):
        for d in deps:
            self.dep(eng, d)
        if sig:
            sem = self.esem[eng]
            sem.v += 1
            tok = (sem, sem.v)
            h = sem.h
            self.q[eng].append(lambda e, m=method, kw=kw, h=h: getattr(e, m)(**kw).then_inc(h, 1))
            return tok
        self.q[eng].append(lambda e, m=method, kw=kw: getattr(e, m)(**kw))
        return None

    def dma(self, eng, out, in_, sem, deps=()):
        for d in deps:
            self.dep(eng, d)
        sem.v += 16
        tok = (sem, sem.v)
        h = sem.h
        self.q[eng].append(lambda e, o=out, i=in_, h=h: e.dma_start(out=o, in_=i).then_inc(h, 16))
        return tok

    def custom(self, eng, fn, deps=()):
        for d in deps:
            self.dep(eng, d)
        self.q[eng].append(fn)

    def emit(self, block):
        for name in ENG:
            ops = self.q[name]
            if not ops:
                continue

            def body(e, ops=ops):
                for f in ops:
                    f(e)
            getattr(block, name)(body)


def build(stop=None):
    nc = bass.Bass("TRN2", target_bir_lowering=False)

    def din(name, shape, dt=F32):
        return nc.dram_tensor(name, list(shape), dt, kind="ExternalInput").ap()

    x_in = din("x", [TH, D])
    w_in = din("w_in", [D, 4096])
    w_pool = din("w_pool", [4, 256, 256])
    pscale = din("pool_scale", [8, 128])
    w_out = din("w_out", [D, D])
    w_ff1 = din("w_ff1", [D, DFF])
    b_ff1 = din("b_ff1", [64, 128])
    w_ff2 = din("w_ff2", [DFF, D])
    vecs = {n: din(n, [1, D]) for n in ("ln_in_g", "ln_in_b", "ln1_g", "ln1_b", "b_ff2", "ln2_g", "ln2_b")}
    ident_d = din("ident", [128, 128])
    identb_d = din("identb", [128, 128], BF16)
    mask_d = din("maskneg", [128, 512], BF16)
    rc_d = din("rc", [128, 512])
    hm_d = din("hm", [128, 128])
    out_d = nc.dram_tensor("out", [T, D], F32, kind="ExternalOutput").ap()
    kvl = [nc.dram_tensor("kvl0", [512, 1024], BF16), nc.dram_tensor("kvl1", [512, 1024], BF16),
           nc.dram_tensor("kvl2", [1024, 512], BF16), nc.dram_tensor("kvl3", [1024, 512], BF16)]
    kva = [nc.dram_tensor("kva0", [2048, 1024], BF16), nc.dram_tensor("kva1", [2048, 1024], BF16),
           nc.dram_tensor("kva2", [4096, 512], BF16), nc.dram_tensor("kva3", [4096, 512], BF16)]
    dbg_d = nc.dram_tensor("dbg", [128, 16 * TH], BF16, kind="ExternalOutput").ap() if stop else None

    es = ExitStack()
    with es:
        arena = es.enter_context(nc.sbuf_tensor("arena", [128, ARENA_BYTES // 2], BF16))
        banks = [es.enter_context(nc.psum_tensor(f"ps{k}", [128, 512], F32))[:, :] for k in range(8)]

        def vw(off, nelem, dt, pat=None, **kw):
            if dt == F32:
                a = arena[:, off // 2: off // 2 + nelem * 2].bitcast(F32)
            else:
                a = arena[:, off // 2: off // 2 + nelem]
            if pat:
                a = a.rearrange(pat, **kw)
            return a

        P = Prog(nc, es)
        block = es.enter_context(nc.Block())

        def finish(deps, dbg_src=None):
            fsem = P.sem("fin")
            tk = None
            for i in range(8):
                tk = P.dma('sync', out_d[i * 128:(i + 1) * 128, :], hres[:, i, :], fsem, deps=deps)
            if dbg_src is not None:
                n = dbg_src.shape[1]
                tk = P.dma('sync', dbg_d[:, 0:n], dbg_src, fsem)
            P.dep('sync', tk)
            for m_ in range(wst['next_take'], wst['next_load']):
                P.dep('sync', wst['loads'][m_][1])
            P.emit(block)
            return nc

        hres = vw(OFF_RES, NT * D, F32, "p (i d) -> p i d", i=NT)
        c0 = OFF_CONST
        identf = vw(c0, 128, F32)
        identb = vw(c0 + 512, 128, BF16)
        maskb = vw(c0 + 768, 512, BF16)
        rc_flat = vw(c0 + 1792, 512, F32)
        rc = rc_flat.rearrange("p (g m) -> p g m", g=4)
        hm_flat = vw(c0 + 3840, 128, F32)
        hm = hm_flat.rearrange("p (i m) -> p i m", i=8)
        b1T = vw(c0 + 4352, 64, F32)
        psT = vw(c0 + 4608, 8, F32)
        epsT = vw(c0 + 4640, 1, F32)
        statf = [vw(c0 + 4704 + k * 128, 24, F32) for k in range(2)]
        stat = [a.rearrange("p (a b) -> p a b", a=4) for a in statf]
        mv = [vw(c0 + 4960 + k * 16, 2, F32) for k in range(2)]
        stdv = [vw(c0 + 4992 + k * 16, 1, F32) for k in range(2)]
        rstd = [vw(c0 + 5024 + k * 16, 1, F32) for k in range(2)]
        nmr = [vw(c0 + 5056 + k * 16, 1, F32) for k in range(2)]
        bst = vw(c0 + 5728, 128, F32)
        pst = vw(c0 + 6240, 128, F32)
        wslot = [vw(OFF_W + k * WSLOT, WSLOT // 2, BF16) for k in range(NWSLOT)]
        mixT = vw(OFF_MIX, 16 * T, BF16, "p (c t) -> p c t", c=16)
        h1T = mixT
        h0T = vw(OFF_X, 16 * TH, BF16, "p (c t) -> p c t", c=16)
        xin = vw(Z1, D, F32)
        PA = vw(Z1 + 8192, D, F32)
        PB = vw(Z1 + 16384, D, F32)
        kst = [vw(Z1 + 24576 + k * 2048, 1024, BF16) for k in range(2)]
        vst = [vw(Z1 + 28672 + k * 1024, 512, BF16) for k in range(2)]
        U = [vw(Z1 + k * 4608, 8 * 144, F32, "p (i m) -> p i m", i=8) for k in range(2)]
        SA = vw(Z1 + 9216, 8 * 144, F32, "p (i m) -> p i m", i=8)
        SB = vw(Z1 + 13824, 8 * 144, F32, "p (i m) -> p i m", i=8)
        ypref = [vw(Z1 + 18432 + k * 2048, 1024, BF16) for k in range(2)]
        ypre = [a.rearrange("p (i m) -> p i m", i=8) for a in ypref]
        wp = vw(208384, 2048, BF16, "p (g k d) -> p g k d", g=4, k=2)
        qT = vw(Z1, 8 * T, BF16, "p (h t) -> p h t", h=8)
        KT = [vw(OFF_X + k * 8192, 4096, BF16, "p (j t) -> p j t", j=4) for k in range(2)]
        VH = [vw(OFF_X + 16384 + k * 8192, 4096, BF16, "p (j i d) -> p j i d", j=4, i=8) for k in range(2)]
        Abuf = [vw(OFF_X + 32768 + k * 1024, 512, BF16) for k in range(3)]
        ATs = [vw(OFF_X + 35840, 512, BF16)] + [vw(Z1 + 16384 + k * 1024, 512, BF16) for k in range(2)]
        nbx = [vw(Z1 + 18432 + k * 2064, 513, F32) for k in range(3)]
        pf = [vw(Z1 + 24624 + k * 2064, 513, F32) for k in range(3)]
        PA3 = vw(OFF_X, D, F32)
        PB3 = vw(OFF_X + 8192, D, F32)
        PC3 = vw(OFF_X + 16384, D, F32)
        gT = [vw(OFF_X + 24576 + k * 16384, 8 * T, BF16, "p (c t) -> p c t", c=8) for k in range(2)]
        rtmp = [vw(OFF_X + 57344 + k * 2048, 512, F32) for k in range(3)]

        bank_free = [None] * 8
        rr = {'n': 0}

        def alloc_bank(subset=None):
            subset = subset or list(range(8))
            k = subset[rr['n'] % len(subset)]
            rr['n'] += 1
            P.dep('tensor', bank_free[k])
            return k

        wsem = [P.sem("w") for _ in range(3)]
        wslot3 = wslot + [vw(OFF_MIX + 16384, WSLOT // 2, BF16)]
        wst = {'next_load': 0, 'next_take': 0, 'loads': {}, 'done': {}}

        def win_src(cg):
            return w_in[:, cg * 512:(cg + 1) * 512].rearrange("(c p) e -> p c e", p=128)

        csem = P.sem("c")
        P.dma('sync', identf, ident_d, csem)
        P.dma('sync', identb, identb_d, csem)
        P.dma('sync', maskb, mask_d, csem)
        P.dma('sync', rc_flat, rc_d, csem)
        P.dma('sync', hm_flat, hm_d, csem)
        P.dma('sync', bst[0:64, :], b_ff1, csem)
        P.dma('sync', pst[0:8, :], pscale, csem)
        P.dma('sync', PA, vecs["ln_in_g"][0].partition_broadcast(128), csem)
        ctok = P.dma('sync', PB, vecs["ln_in_b"][0].partition_broadcast(128), csem)
        xsem = [P.sem("x") for _ in range(9)]
        xtok = [None] * 9
        order = [0, 1, 2, 3, 4, 5, 6, 7, 8]
        for i in order:
            dst = hres[:, i, :] if i < 8 else xin
            xtok[i] = P.dma('sync', dst, x_in[i * 128:(i + 1) * 128, :], xsem[i])

        te = P.op('vector', 'memset', ap=epsT, constant=EPS)
        kb = alloc_bank()
        tb = P.op('tensor', 'transpose', deps=[ctok], out=banks[kb][:, 0:64], in_=bst[0:64, :], identity=identf[0:64, 0:64])
        t1 = P.op('vector', 'tensor_copy', deps=[tb], out=b1T, in_=banks[kb][:, 0:64])
        bank_free[kb] = t1
        kb = alloc_bank()
        tb = P.op('tensor', 'transpose', out=banks[kb][:, 0:8], in_=pst[0:8, :], identity=identf[0:8, 0:8])
        t1 = P.op('vector', 'tensor_copy', deps=[tb], out=psT, in_=banks[kb][:, 0:8])
        bank_free[kb] = t1

        def ln_stats(src, k, deps):
            toks = []
            for c in range(4):
                toks.append(P.op('vector', 'bn_stats', deps=deps, out=stat[k][:, c, :], in_=src[:, c * 512:(c + 1) * 512]))
            ta = P.op('vector', 'bn_aggr', deps=[toks[-1]], out=mv[k], in_=statf[k])
            ts = P.op('scalar', 'activation', deps=[ta, te], out=stdv[k], in_=mv[k][:, 1:2], func=AF.Sqrt, bias=epsT, scale=1.0)
            tr = P.op('vector', 'reciprocal', deps=[ts], out=rstd[k], in_=stdv[k])
            tn = P.op('vector', 'tensor_scalar', deps=[tr], out=nmr[k], in0=mv[k][:, 0:1], scalar1=rstd[k], scalar2=-1.0,
                      op0=ALU.mult, op1=ALU.mult)
            return tn

        def ln_apply(src, k, tn, G, Bt, deps):
            t = P.op('scalar', 'activation', deps=[tn] + list(deps), out=src, in_=src, func=AF.Identity, bias=nmr[k], scale=rstd[k])
            t = P.op('gpsimd', 'tensor_tensor', deps=[t], out=src, in0=src, in1=G, op=ALU.mult)
            t = P.op('vector', 'tensor_tensor', deps=[t], out=src, in0=src, in1=Bt, op=ALU.add)
            return t

        class LNPipe:
            def __init__(self, n, src_fn, dep_fn, G, Bt, par_deps, post_fn):
                self.n, self.src_fn, self.dep_fn, self.G, self.Bt = n, src_fn, dep_fn, G, Bt
                self.par_deps, self.post_fn = list(par_deps), post_fn
                self.tn = [None] * n
                self.tg = [None] * n
                self.tb = [None] * n
                self.s = 0

            def step(self):
                step, n = self.s, self.n
                if step >= n + 3:
                    return
                self.s += 1
                if 0 <= step - 1 < n:
                    i = step - 1
                    k = i % 2
                    src = self.src_fn(i)
                    t = P.op('scalar', 'activation', deps=[self.tn[i]] + self.par_deps, out=src, in_=src, func=AF.Identity,
                             bias=nmr[k], scale=rstd[k])
                    self.tg[i] = P.op('gpsimd', 'tensor_tensor', deps=[t], out=src, in0=src, in1=self.G, op=ALU.mult)
                if step < n:
                    self.tn[step] = ln_stats(self.src_fn(step), step % 2, self.dep_fn(step))
                if 0 <= step - 2 < n:
                    i = step - 2
                    src = self.src_fn(i)
                    self.tb[i] = P.op('vector', 'tensor_tensor', deps=[self.tg[i]], out=src, in0=src, in1=self.Bt, op=ALU.add)
                if 0 <= step - 3 < n:
                    self.post_fn(step - 3, self.tb[step - 3])

            def finish(self):
                while self.s < self.n + 3:
                    self.step()
                return self.tb

        def ln_pipeline(n, src_fn, dep_fn, G, Bt, par_deps, post_fn):
            return LNPipe(n, src_fn, dep_fn, G, Bt, par_deps, post_fn).finish()

        def transposes(src, dstT, col0, ncols, deps, evac_dep=None):
            toks = []
            tpe = None
            for b4 in range(4):
                kb = alloc_bank()
                for s in range(4):
                    c = b4 * 4 + s
                    tpe = P.op('tensor', 'transpose', deps=deps, sig=(s == 3), out=banks[kb][:, s * 128:(s + 1) * 128],
                               in_=src[:, c * 128:(c + 1) * 128], identity=identf)
                eng = 'scalar' if b4 % 2 == 0 else 'vector'
                o = dstT[:, b4 * 4:(b4 + 1) * 4, col0:col0 + 128]
                i_ = banks[kb].rearrange("p (s m) -> p s m", s=4)
                dd = [tpe] + ([evac_dep] if evac_dep else [])
                if eng == 'scalar':
                    t = P.op('scalar', 'activation', deps=dd, out=o, in_=i_, func=AF.Copy)
                else:
                    t = P.op('vector', 'tensor_copy', deps=dd, out=o, in_=i_)
                bank_free[kb] = t
                toks.append(t)
            return toks, tpe

        cg_order = [4, 5, 6, 7, 0, 1, 2, 3]
        all_w = [win_src(cg) for cg in cg_order]
        all_w += [w_out[:, cg * 512:(cg + 1) * 512].rearrange("(c p) e -> p c e", p=128) for cg in range(4)]
        for f8 in range(8):
            for a in range(2):
                c0_ = f8 * 1024 + a * 512
                all_w.append(w_ff1[:, c0_:c0_ + 512].rearrange("(c p) e -> p c e", p=128))
            for hlf in range(2):
                all_w.append(w_ff2[f8 * 1024:(f8 + 1) * 1024, hlf * 1024:(hlf + 1) * 1024].rearrange("(c p) e -> p c e", p=128))
        all_shapes = [(16, 512)] * 12 + [(16, 512), (16, 512), (8, 1024), (8, 1024)] * 8
        NL = len(all_w)
        slot_seq = [0, 1, 2, 0, 1, 2, 0, 1] + [m % 2 for m in range(8, NL)]
        prev_user = []
        for m in range(NL):
            pu = -1
            for m2 in range(m - 1, -1, -1):
                if slot_seq[m2] == slot_seq[m]:
                    pu = m2
                    break
            prev_user.append(pu)

        def issue_ready(n_done):
            while wst['next_load'] < NL and prev_user[wst['next_load']] <= n_done:
                m = wst['next_load']
                k = slot_seq[m]
                a, b = all_shapes[m]
                dst = wslot3[k].rearrange("p (a b) -> p a b", a=a)
                tok = P.dma('gpsimd', dst, all_w[m], wsem[k], deps=[wst['done'].get(prev_user[m])])
                wst['loads'][m] = (dst, tok, m)
                wst['next_load'] += 1

        def take_weight():
            m = wst['next_take']
            wst['next_take'] += 1
            return wst['loads'][m]

        def wrelease(m, tok):
            wst['done'][m] = tok
            wst['last'] = m

        def prefetch():
            issue_ready(wst['last'])

        issue_ready(-1)
        wpsem = P.sem("wp")
        wptok = P.dma('gpsimd', wp, w_pool.rearrange("g (k p) d -> p g k d", p=128), wpsem)

        h0T_ready = [None] * 9
        tn_l = [None] * 9
        la_l = [None] * 9
        tile_order = [0, 1, 2, 3, 4, 5, 6, 7, 8]
        def _src0(i):
            return hres[:, i, :] if i < 8 else xin

        def _post0(i, tb_):
            col0 = i * 128 if i < 8 else 1024
            h0T_ready[i], _ = transposes(_src0(i), h0T, col0, 128, [tb_, ctok])

        ln_pipeline(9, _src0, lambda i: [xtok[i], ctok], PA, PB, [], _post0)

        all_h0T = [t for lst in h0T_ready for t in lst]
        if stop == 'ln':
            return finish(all_h0T, vw(OFF_X, 16 * TH, BF16))

        kvsem = [P.sem("kv") for _ in range(4)]
        kst_free = [None, None]
        vst_free = [None, None]
        kv_tokens = []
        pe_last = None

        ccsem = P.sem("cc")
        cctoks = [None] * 4

        def allgather(k, deps):
            for d in deps:
                P.dep('gpsimd', d)
            ccsem.v += 1
            cctoks[k] = (ccsem, ccsem.v)
            P.custom('gpsimd', lambda e, h=ccsem.h, k=k: e.collective_compute(
                "AllGather", ALU.bypass, replica_groups=[[0, 1, 2, 3], [4, 5, 6, 7]],
                ins=[kvl[k].ap().opt()], outs=[kva[k].ap().opt()], dma_qos="P3").then_inc(h, 1))

        for cg in (4, 5):
            W, wtok, wk = take_weight()
            for ec in range(4):
                bk = [alloc_bank(), alloc_bank()]
                for dc in range(16):
                    for th in range(2):
                        deps = [wtok] + (h0T_ready[th * 4] + h0T_ready[th * 4 + 1] + h0T_ready[th * 4 + 2] + h0T_ready[th * 4 + 3]
                                         if dc == 0 else [])
                        pe_last = P.op('tensor', 'matmul', deps=deps, sig=(dc == 15), out=banks[bk[th]][:, :],
                                       lhsT=W[:, dc, ec * 128:(ec + 1) * 128], rhs=h0T[:, dc, th * 512:(th + 1) * 512],
                                       start=(dc == 0), stop=(dc == 15))
                        if dc == 15:
                            bank_free[bk[th]] = None
                            kk = (cg - 4) * 4 + ec
                            s = kk % 2
                            if th == 0:
                                t = P.op('scalar', 'activation', deps=[pe_last, kst_free[s]], out=kst[s][:, 0:512], in_=banks[bk[0]][:, :], func=AF.Copy)
                                t_a = t
                                bank_free[bk[0]] = t
                            else:
                                t = P.op('vector', 'tensor_copy', deps=[pe_last, kst_free[s]], out=kst[s][:, 512:1024], in_=banks[bk[1]][:, :])
                                bank_free[bk[1]] = t
                                row0 = ec * 128
                                tk = P.dma('sync', kvl[cg - 4][row0:row0 + 128, :], kst[s], kvsem[s], deps=[t, t_a])
                                kst_free[s] = tk
                                kv_tokens.append(tk)
            wrelease(wk, pe_last)
            allgather(cg - 4, [kst_free[0], kst_free[1]])
            prefetch()

        for cg in (6, 7):
            W, wtok, wk = take_weight()
            for tt in range(8):
                bk = alloc_bank()
                for dc in range(16):
                    pe_last = P.op('tensor', 'matmul', deps=[wtok] + (all_h0T if dc == 0 else []), sig=(dc == 15), out=banks[bk][:, :],
                                   lhsT=h0T[:, dc, tt * 128:(tt + 1) * 128], rhs=W[:, dc, :], start=(dc == 0), stop=(dc == 15))
                s = tt % 2
                if s == 0:
                    t = P.op('scalar', 'activation', deps=[pe_last, vst_free[s]], out=vst[s], in_=banks[bk][:, :], func=AF.Copy)
                else:
                    t = P.op('vector', 'tensor_copy', deps=[pe_last, vst_free[s]], out=vst[s], in_=banks[bk][:, :])
                bank_free[bk] = t
                tk = P.dma('sync', kvl[cg - 4][tt * 128:(tt + 1) * 128, :], vst[s], kvsem[2 + s], deps=[t])
                vst_free[s] = tk
                kv_tokens.append(tk)
            wrelease(wk, pe_last)
            allgather(cg - 4, [vst_free[0], vst_free[1]])
            prefetch()

        if stop == 'kv0':
            return finish([pe_last] + kv_tokens[-4:] + [kst_free[0], kst_free[1]])
        if stop == 'kv':
            for k in range(4):
                P.dep('sync', cctoks[k])
            return finish([pe_last] + kv_tokens[-4:])
        ypre_free = [None, None]
        U_free = [None, None]
        ypre_tok = [None, None]
        pend = {'g': None}

        def pool_linear():
            if pend['g'] is None:
                return
            g, tk0, tk1 = pend['g']
            pend['g'] = None
            pe2 = None
            for oc in range(2):
                bk2 = [alloc_bank(), alloc_bank()]
                for k2 in range(2):
                    for th in range(2):
                        pe2 = P.op('tensor', 'matmul', deps=[wptok, tk0, tk1], sig=(k2 == 1), out=banks[bk2[th]][:, :],
                                   lhsT=wp[:, g, k2, oc * 128:(oc + 1) * 128],
                                   rhs=ypref[k2][:, th * 512:(th + 1) * 512],
                                   start=(k2 == 0), stop=(k2 == 1))
                        if k2 == 1:
                            tev = P.op('scalar', 'activation', deps=[pe2], out=mixT[:, g * 2 + oc, th * 512:(th + 1) * 512],
                                       in_=banks[bk2[th]][:, :], func=AF.Identity, scale=psT[:, g * 2 + oc:g * 2 + oc + 1])
                            bank_free[bk2[th]] = tev
                            pend['tev'] = tev
            ypre_free[0] = pe2
            ypre_free[1] = pe2
            pend['pe'] = pe2

        for cg in (0, 1):
            W, wtok, wk = take_weight()
            for ec in range(4):
                c = cg * 4 + ec
                g = c // 2
                kc = c % 2
                w = WINDOWS[g]
                bk = [alloc_bank(), alloc_bank(), alloc_bank()]
                for dc in range(16):
                    for th in range(3):
                        n0, n1 = (th * 512, th * 512 + 512) if th < 2 else (1024, 1152)
                        ob = banks[bk[th]][:, :] if th < 2 else banks[bk[2]][:, 0:128]
                        pe_last = P.op('tensor', 'matmul', deps=[wtok] + (all_h0T if dc == 0 else []), sig=(dc == 15 and th == 2), out=ob,
                                       lhsT=W[:, dc, ec * 128:(ec + 1) * 128], rhs=h0T[:, dc, n0:n1],
                                       start=(dc == 0), stop=(dc == 15))
                pool_linear()
                u = U[c % 2]
                t0 = P.op('scalar', 'activation', deps=[pe_last, U_free[c % 2]], out=u[:, 0:4, 0:128],
                          in_=banks[bk[0]].rearrange("p (s m) -> p s m", s=4), func=AF.Copy)
                bank_free[bk[0]] = t0
                t1 = P.op('vector', 'tensor_copy', deps=[pe_last, U_free[c % 2]], out=u[:, 4:8, 0:128],
                          in_=banks[bk[1]].rearrange("p (s m) -> p s m", s=4))
                bank_free[bk[1]] = t1
                t2 = P.op('vector', 'tensor_tensor', out=u[:, :, 128:144],
                          in0=banks[bk[2]][:, 0:128].rearrange("p (i m) -> p i m", i=8), in1=hm, op=ALU.mult)
                bank_free[bk[2]] = t2
                t = P.op('gpsimd', 'tensor_tensor', deps=[t0, t2, U_free[(c + 1) % 2]], out=SA[:, :, 0:143], in0=u[:, :, 0:143], in1=u[:, :, 1:144], op=ALU.add)
                cur, oth = SA, SB
                if w >= 4:
                    t = P.op('gpsimd', 'tensor_tensor', deps=[t], out=oth[:, :, 0:141], in0=cur[:, :, 0:141], in1=cur[:, :, 2:143], op=ALU.add)
                    cur, oth = oth, cur
                if w >= 8:
                    t = P.op('gpsimd', 'tensor_tensor', deps=[t], out=oth[:, :, 0:137], in0=cur[:, :, 0:137], in1=cur[:, :, 4:141], op=ALU.add)
                    cur, oth = oth, cur
                if w >= 16:
                    t = P.op('gpsimd', 'tensor_tensor', deps=[t], out=oth[:, :, 0:129], in0=cur[:, :, 0:129], in1=cur[:, :, 8:137], op=ALU.add)
                    cur, oth = oth, cur
                yp = ypre[kc]
                ta = P.op('gpsimd', 'tensor_scalar', deps=[t], out=oth[:, 0:7, 0:128], in0=cur[:, 0:7, 0:128], scalar1=1.0 / w,
                          scalar2=1.0, op0=ALU.mult, op1=ALU.mult)
                tb_ = P.op('gpsimd', 'tensor_tensor', deps=[t], out=oth[:, 7, 0:128], in0=cur[:, 7, 0:128], in1=rc[:, g, :], op=ALU.mult)
                tc_ = P.op('gpsimd', 'tensor_tensor', deps=[ta, tb_, ypre_free[kc]], out=yp[:, :, :], in0=oth[:, :, 0:128], in1=u[:, :, 0:128],
                           op=ALU.subtract)
                U_free[c % 2] = tc_
                ypre_tok[kc] = tc_
                if kc == 1:
                    pend['g'] = (g, ypre_tok[0], ypre_tok[1])
            wrelease(wk, pe_last)
            prefetch()
        pool_linear()
        pool_pe_last = pend['pe']

        if stop == 'pool':
            return finish([pool_pe_last, pend['tev']], vw(OFF_MIX, 16 * T, BF16))
        q_evac = []
        for cg in (2, 3):
            W, wtok, wk = take_weight()
            for ec in range(4):
                hh = (cg - 2) * 4 + ec
                bk = [alloc_bank(), alloc_bank()]
                for dc in range(16):
                    for th in range(2):
                        pe_last = P.op('tensor', 'matmul', deps=[wtok], sig=(dc == 15), out=banks[bk[th]][:, :],
                                       lhsT=W[:, dc, ec * 128:(ec + 1) * 128], rhs=h0T[:, dc, th * 512:(th + 1) * 512],
                                       start=(dc == 0), stop=(dc == 15))
                        if dc == 15:
                            if th == 0:
                                t = P.op('scalar', 'activation', deps=[pe_last, pool_pe_last], out=qT[:, hh, 0:512], in_=banks[bk[0]][:, :], func=AF.Copy)
                            else:
                                t = P.op('vector', 'tensor_copy', deps=[pe_last, pool_pe_last], out=qT[:, hh, 512:1024], in_=banks[bk[1]][:, :])
                            bank_free[bk[th]] = t
                            q_evac.append(t)
            wrelease(wk, pe_last)
            prefetch()
        inproj_pe_last = pe_last

        if stop == 'q':
            return finish(q_evac, vw(Z1, 8 * T, BF16))
        P.new_phase()
        ksem = [P.sem("k") for _ in range(2)]
        vsem = [P.sem("v") for _ in range(2)]
        kvh_free = [None, None]
        kva_k = [kva[k].ap().rearrange("(j r) c -> r j c", j=4) for k in range(2)]
        kva_v = [kva[2 + k].ap().rearrange("(j i p) c -> p j i c", j=4, i=8, p=128) for k in range(2)]

        def load_head(h):
            s = h % 2
            d = [cctoks[h // 4], cctoks[2 + h // 4], inproj_pe_last, kvh_free[s]] + (q_evac if h < 2 else [])
            hl = h % 4
            tk_ = P.dma('sync', KT[s], kva_k[h // 4][hl * 128:(hl + 1) * 128, :, :], ksem[s], deps=d)
            for j4 in range(4):
                tv_ = P.dma('sync', VH[s][:, j4, :, :], kva_v[h // 4][:, j4, :, hl * 128:(hl + 1) * 128], vsem[s])
            return tk_, tv_

        for k in range(3):
            tm = P.op('vector', 'memset', deps=q_evac, ap=nbx[k][:, 0:1], constant=1.0)
        nb_init = tm

        head_tok = {0: load_head(0), 1: load_head(1)}
        pieces = []
        for h in range(8):
            for iq in range(8):
                for i in range(iq, 8):
                    pieces.append((h, iq, i))
        NP_ = len(pieces)
        ZB = [0, 1, 2, 7]
        ATB = [3, 6]
        YB = [4, 5]
        st = {}
        nb_free = [None] * 3
        pf_free = [None] * 3
        A_free = [None] * 3
        ATs_free = [None] * 3
        atps_free = [None, None]
        ybank_of = {}
        y_cnt = {'n': 0}
        atps = [banks[ATB[0]].bitcast(BF16), banks[ATB[1]].bitcast(BF16)]
        P.dep('tensor', bank_free[ATB[0]])
        P.dep('tensor', bank_free[ATB[1]])
        mix_tok = []
        last_av_of_head = {}

        def S123(n):
            h, iq, i = pieces[n]
            first = (i == iq)
            s3 = n % 3
            zb = ZB[n % 4]
            P.dep('tensor', bank_free[zb])
            ktok, vtok = head_tok[h]
            tq = P.op('tensor', 'matmul', deps=[ktok] + q_evac, sig=(not first), out=banks[zb][:, :],
                      lhsT=qT[:, h, iq * 128:(iq + 1) * 128], rhs=KT[h % 2][:, :, i * 128:(i + 1) * 128],
                      start=True, stop=(not first))
            if first:
                tq = P.op('tensor', 'matmul', out=banks[zb][:, :], lhsT=identb, rhs=maskb, start=False, stop=True)
            tsg = P.op('scalar', 'activation', deps=[tq, nb_free[s3], nb_init], out=nbx[s3][:, 1:513], in_=banks[zb][:, :],
                       func=AF.Sigmoid, scale=-QSCALE)
            bank_free[zb] = tsg
            init = 1.0 if first else pf[(n - 1) % 3][:, 512:513]
            dd = [tsg, pf_free[s3]]
            if not first:
                dd.append(st[n - 1]['scan'])
            tsc = P.op('vector', 'tensor_tensor_scan', deps=dd, out=pf[s3], data0=nbx[s3], data1=nbx[s3], initial=init,
                       op0=ALU.mult, op1=ALU.bypass)
            nb_free[s3] = tsc
            st[n] = {'scan': tsc}
            if n >= 1:
                Sdiff(n - 1)

        def Sdiff(n):
            s3 = n % 3
            tdf = P.op('vector', 'tensor_tensor', deps=[st[n]['scan'], A_free[s3]], out=Abuf[s3], in0=pf[s3][:, 0:512],
                       in1=pf[s3][:, 1:513], op=ALU.subtract)
            st[n]['diff'] = tdf
            pf_free[s3] = tdf

        def S45(n):
            h, iq, i = pieces[n]
            s3 = n % 3
            hf = n % 2
            P.dep('tensor', atps_free[hf])
            for jj in range(4):
                tt_ = P.op('tensor', 'transpose', deps=[st[n]['diff']], sig=(jj == 3),
                           out=atps[hf][:, jj * 128:(jj + 1) * 128],
                           in_=Abuf[s3][:, jj * 128:(jj + 1) * 128], identity=identb)
            A_free[s3] = tt_
            tcp = P.op('scalar', 'activation', deps=[tt_, ATs_free[s3]], out=ATs[s3], in_=atps[hf][:, 0:512], func=AF.Copy)
            atps_free[hf] = tcp
            st[n]['at'] = tcp

        def S6(n):
            h, iq, i = pieces[n]
            s3 = n % 3
            first = (i == iq)
            last = (i == 7)
            if first:
                yb = YB[y_cnt['n'] % 2]
                y_cnt['n'] += 1
                ybank_of[(h, iq)] = yb
                P.dep('tensor', bank_free[yb])
            yb = ybank_of[(h, iq)]
            ktok, vtok = head_tok[h]
            for jj in range(4):
                tav = P.op('tensor', 'matmul', deps=[st[n]['at'], vtok], sig=(jj == 3), out=banks[yb][:, 0:128],
                           lhsT=VH[h % 2][:, jj, i, :], rhs=ATs[s3][:, jj * 128:(jj + 1) * 128],
                           start=(first and jj == 0), stop=(last and jj == 3))
            ATs_free[s3] = tav
            if last:
                tm_ = P.op('scalar', 'activation', deps=[tav], out=mixT[:, 8 + h, iq * 128:(iq + 1) * 128], in_=banks[yb][:, 0:128], func=AF.Copy)
                bank_free[yb] = tm_
                mix_tok.append(tm_)
                if iq == 7:
                    last_av_of_head[h] = tav
                    kvh_free[h % 2] = tav
                    if h + 2 < 8:
                        head_tok[h + 2] = load_head(h + 2)
            del st[n]['at']

        for n in range(NP_ + 4):
            if n < NP_:
                S123(n)
            elif n == NP_:
                Sdiff(NP_ - 1)
            if 0 <= n - 2 < NP_:
                S45(n - 2)
            if 0 <= n - 4 < NP_:
                S6(n - 4)

        bank_free[ATB[0]] = atps_free[0]
        bank_free[ATB[1]] = atps_free[1]
        if stop == 'attn':
            return finish(mix_tok[-8:], vw(OFF_MIX, 16 * T, BF16))
        P.new_phase()
        p3sem = P.sem("p3")
        att_done = mix_tok[-1]
        tg1 = P.dma('sync', PA3, vecs["ln1_g"][0].partition_broadcast(128), p3sem, deps=[att_done, last_av_of_head[7], last_av_of_head[6]])
        tb1 = P.dma('sync', PB3, vecs["ln1_b"][0].partition_broadcast(128), p3sem)
        tb2 = P.dma('sync', PC3, vecs["b_ff2"][0].partition_broadcast(128), p3sem)
        partok = tb2
        res_tok = [[None] * 4 for _ in range(8)]
        h1T_ready = [None] * 8
        acc_tok = [None] * 8
        tp_l = [None] * 8
        tile_pe = [None] * 8

        def _post1(i, tb_):
            h1T_ready[i], tp_l[i] = transposes(hres[:, i, :], h1T, i * 128, 128, [tb_, tile_pe[i]], evac_dep=tile_pe[i])
            acc_tok[i] = P.op('vector', 'scalar_tensor_tensor', deps=[tp_l[i], partok], out=hres[:, i, :], in0=hres[:, i, :],
                              scalar=ALPHA, in1=PC3, op0=ALU.mult, op1=ALU.add)

        ln1 = LNPipe(8, lambda i: hres[:, i, :], lambda i: res_tok[i], PA3, PB3, [partok], _post1)
        for cg in range(4):
            W, wtok, wk = take_weight()
            for tt in range(8):
                bk = alloc_bank()
                for ec in range(16):
                    pe_last = P.op('tensor', 'matmul', deps=[wtok] + (mix_tok if ec == 0 else []), sig=(ec == 15), out=banks[bk][:, :],
                                   lhsT=mixT[:, ec, tt * 128:(tt + 1) * 128], rhs=W[:, ec, :], start=(ec == 0), stop=(ec == 15))
                t = P.op('vector', 'scalar_tensor_tensor', deps=[pe_last], out=hres[:, tt, cg * 512:(cg + 1) * 512],
                         in0=hres[:, tt, cg * 512:(cg + 1) * 512], scalar=ALPHA, in1=banks[bk][:, :], op0=ALU.mult, op1=ALU.add)
                bank_free[bk] = t
                res_tok[tt][cg] = t
                if cg == 3:
                    tile_pe[tt] = pe_last
                    ln1.step()
            wrelease(wk, pe_last)
            prefetch()
        outproj_pe_last = pe_last
        la_l = ln1.finish()
        all_h1T = [t for lst in h1T_ready for t in lst]

        if stop == 'p3':
            return finish(acc_tok + all_h1T, vw(OFF_MIX, 16 * T, BF16))
        P.new_phase()
        p4sem = P.sem("p4")
        gT_free = [None, None]
        rt_free = [None] * 3
        tg2 = P.dma('sync', PA3, vecs["ln2_g"][0].partition_broadcast(128), p4sem, deps=[la_l[7]])
        tb2_ = P.dma('sync', PB3, vecs["ln2_b"][0].partition_broadcast(128), p4sem)
        osem = P.sem("o")
        otoks = []

        def _post2(i, tb_):
            otoks.append(P.dma('sync', out_d[i * 128:(i + 1) * 128, :], hres[:, i, :], osem, deps=[tb_]))

        ln2 = None
        rcnt = {'n': 0}
        last_acc = [[acc_tok[tt]] * 4 for tt in range(8)]
        for f8 in range(8):
            g = gT[f8 % 2]
            g_toks = []
            for a in range(2):
                W, wtok, wk = take_weight()
                for fc in range(4):
                    fchunk = f8 * 8 + a * 4 + fc
                    bk = [alloc_bank(), alloc_bank()]
                    for dc in range(16):
                        for th in range(2):
                            pe_last = P.op('tensor', 'matmul', deps=[wtok] + (all_h1T if dc == 0 else []), sig=(dc == 15), out=banks[bk[th]][:, :],
                                           lhsT=W[:, dc, fc * 128:(fc + 1) * 128], rhs=h1T[:, dc, th * 512:(th + 1) * 512],
                                           start=(dc == 0), stop=(dc == 15))
                            if dc == 15:
                                r = rcnt['n'] % 3
                                rcnt['n'] += 1
                                t = P.op('scalar', 'activation', deps=[pe_last, rt_free[r]], out=rtmp[r], in_=banks[bk[th]][:, :], func=AF.Relu,
                                         bias=b1T[:, fchunk:fchunk + 1], scale=1.0)
                                bank_free[bk[th]] = t
                                t2 = P.op('vector', 'tensor_tensor', deps=[t, gT_free[f8 % 2]], out=g[:, a * 4 + fc, th * 512:(th + 1) * 512],
                                          in0=rtmp[r], in1=rtmp[r], op=ALU.mult)
                                rt_free[r] = t2
                                g_toks.append(t2)
                wrelease(wk, pe_last)
                prefetch()
            for hlf in range(2):
                W, wtok, wk = take_weight()
                for tt in range(8):
                    for cl in range(2):
                        bk = alloc_bank()
                        cg = hlf * 2 + cl
                        for fc in range(8):
                            pe_last = P.op('tensor', 'matmul', deps=[wtok] + (g_toks if fc == 0 else []), sig=(fc == 7), out=banks[bk][:, :],
                                           lhsT=g[:, fc, tt * 128:(tt + 1) * 128], rhs=W[:, fc, cl * 512:(cl + 1) * 512],
                                           start=(fc == 0), stop=(fc == 7))
                        t = P.op('vector', 'tensor_tensor', deps=[pe_last, last_acc[tt][cg]], out=hres[:, tt, cg * 512:(cg + 1) * 512],
                                 in0=hres[:, tt, cg * 512:(cg + 1) * 512], in1=banks[bk][:, :], op=ALU.add)
                        bank_free[bk] = t
                        last_acc[tt][cg] = t
                    if f8 == 7 and hlf == 1:
                        if ln2 is None:
                            ln2 = LNPipe(8, lambda i: hres[:, i, :], lambda i: last_acc[i], PA3, PB3, [tb2_], _post2)
                        ln2.step()
                wrelease(wk, pe_last)
                prefetch()
            gT_free[f8 % 2] = pe_last

        ln2.finish()
        otok = otoks[-1]
        P.dep('sync', otok)
        P.emit(block)
    return nc


_NC_CACHE = {}


def _host_layout(inputs):
    x = np.asarray(inputs["x"], dtype=np.float32)
    B, S, _ = x.shape
    xr = x[:, ::-1, :]
    f32 = np.float32
    common = {
        "w_in": np.ascontiguousarray(inputs["w_in"][0], dtype=f32),
        "w_pool": np.ascontiguousarray(inputs["w_pool"][0], dtype=f32),
        "pool_scale": np.ascontiguousarray(inputs["pool_scale"][0], dtype=f32).reshape(8, 128),
        "w_out": np.ascontiguousarray(inputs["w_out"][0], dtype=f32),
        "w_ff1": np.ascontiguousarray(inputs["w_ff1"][0], dtype=f32),
        "b_ff1": np.ascontiguousarray(inputs["b_ff1"][0], dtype=f32).reshape(64, 128),
        "w_ff2": np.ascontiguousarray(inputs["w_ff2"][0], dtype=f32),
        "ln_in_g": np.ascontiguousarray(inputs["ln_in_g"], dtype=f32).reshape(1, D),
        "ln_in_b": np.ascontiguousarray(inputs["ln_in_b"], dtype=f32).reshape(1, D),
        "ln1_g": np.ascontiguousarray(inputs["ln1_g"][0], dtype=f32).reshape(1, D),
        "ln1_b": np.ascontiguousarray(inputs["ln1_b"][0], dtype=f32).reshape(1, D),
        "b_ff2": np.ascontiguousarray(inputs["b_ff2"][0], dtype=f32).reshape(1, D),
        "ln2_g": np.ascontiguousarray(inputs["ln2_g"][0], dtype=f32).reshape(1, D),
        "ln2_b": np.ascontiguousarray(inputs["ln2_b"][0], dtype=f32).reshape(1, D),
        "ident": np.eye(128, dtype=f32),
        "identb": np.eye(128, dtype=f32).astype(ml_dtypes.bfloat16),
    }
    in_maps = []
    for c in range(8):
        b, j = c // 4, c % 4
        xc = np.zeros((TH, D), dtype=f32)
        hm = np.zeros((128, 128), dtype=f32)
        for i in range(8):
            g = 4 * i + j
            xc[i * 128:(i + 1) * 128] = xr[b, g * 128:(g + 1) * 128]
            if g + 1 < 32:
                xc[1024 + i * 16:1024 + (i + 1) * 16] = xr[b, (g + 1) * 128:(g + 1) * 128 + 16]
                hm[:, i * 16:(i + 1) * 16] = 1.0
        mask = np.zeros((128, 512), dtype=f32)
        for jj in range(4):
            if jj < j:
                mask[:, jj * 128:(jj + 1) * 128] = MASKVAL
            elif jj == j:
                qq = np.arange(128)[:, None]
                kk = np.arange(128)[None, :]
                mask[:, jj * 128:(jj + 1) * 128] = np.where(kk <= qq, MASKVAL, 0.0)
        rcv = np.zeros((128, 4, 128), dtype=f32)
        g7 = 28 + j
        for gi, w in enumerate(WINDOWS):
            r = g7 * 128 + np.arange(128)
            t = (S - 1) - r
            cnt = np.minimum(t + 1, w).astype(f32)
            rcv[:, gi, :] = (1.0 / cnt)[None, :]
        m = dict(common)
        m["x"] = xc
        m["maskneg"] = mask.astype(ml_dtypes.bfloat16)
        m["rc"] = rcv.reshape(128, 512)
        m["hm"] = hm
        in_maps.append(m)
    return in_maps


def kernel(**inputs):
    if "nc" not in _NC_CACHE:
        _NC_CACHE["nc"] = build()
    nc = _NC_CACHE["nc"]
    in_maps = _host_layout(inputs)
    res = run_bass_kernel_spmd(nc, in_maps, core_ids=list(range(8)))
    x = inputs["x"]
    B, S, _ = x.shape
    out_r = np.zeros((B, S, D), dtype=np.float32)
    for c in range(8):
        b, j = c // 4, c % 4
        oc = np.asarray(res.results[c]["out"], dtype=np.float32)
        for i in range(8):
            g = 4 * i + j
            out_r[b, g * 128:(g + 1) * 128] = oc[i * 128:(i + 1) * 128]
    return np.ascontiguousarray(out_r[:, ::-1, :])
```
